# Optimizing a Trainium2 kernel written in Bass

```python
import jax, jax.numpy as jnp
from jax import lax
import numpy as np

D_MODEL = 1024
BATCH = 4
SEQ = 4096
DEPTH = 4
DEC_BATCH = 4
DEC_SEQ = 8192
PAST_LEN = 128

CONV_WIDTH = D_MODEL
CONV_KERNEL = 31
CONV_PAD = CONV_KERNEL // 2
SGU_WIDTH = D_MODEL
SGU_GROUPS = 8
SGU_GROUP_DIM = SGU_WIDTH // SGU_GROUPS
CHUNK = 128
RMS_EPS = 1e-6
LN_EPS = 1e-5

IN_COLS_LIST = [CONV_WIDTH, CONV_WIDTH, CONV_WIDTH, SGU_WIDTH, SGU_WIDTH, SGU_WIDTH, D_MODEL, D_MODEL]
IN_COLS = int(sum(IN_COLS_LIST))
IN_SPLITS = [int(v) for v in np.cumsum(IN_COLS_LIST)[:-1]]

kernel_name = "hybrid_conv_sgu_gated_encoder"


def rms_norm(x, g):
    xf = x.astype(jnp.float32)
    y = xf * lax.rsqrt(jnp.mean(xf * xf, axis=-1, keepdims=True) + RMS_EPS)
    return (y * g.astype(jnp.float32)).astype(x.dtype)


def layer_norm(x, g, b):
    xf = x.astype(jnp.float32)
    mu = jnp.mean(xf, axis=-1, keepdims=True)
    xc = xf - mu
    var = jnp.mean(xc * xc, axis=-1, keepdims=True)
    y = xc * lax.rsqrt(var + LN_EPS) * g.astype(jnp.float32) + b.astype(jnp.float32)
    return y.astype(x.dtype)


def conv_branch(a_val, a_glu, a_z, conv_w, conv_b, ln_g, ln_b, w_proj):
    h = a_val * jax.nn.sigmoid(a_glu)
    h = lax.conv_general_dilated(
        h, conv_w[:, None, :].astype(h.dtype),
        window_strides=(1,), padding=[(CONV_PAD, CONV_PAD)],
        dimension_numbers=("NWC", "WIO", "NWC"),
        feature_group_count=CONV_WIDTH) + conv_b
    h = layer_norm(h, ln_g, ln_b)
    h = jax.nn.silu(h) * jax.nn.silu(a_z)
    return h @ w_proj


def sgu_branch(u, v, b_z, ln_g, ln_b, w_s, b_s, w_proj):
    u = jax.nn.gelu(u)
    v = layer_norm(jax.nn.gelu(v), ln_g, ln_b)
    bsz, seq, _ = v.shape
    vc = v.reshape(bsz, seq // CHUNK, CHUNK, SGU_GROUPS, SGU_GROUP_DIM)
    mixed = jnp.einsum("gpq,bnqgc->bnpgc", w_s, vc) + jnp.transpose(b_s)[None, None, :, :, None]
    h = u * mixed.reshape(bsz, seq, SGU_WIDTH) * jax.nn.silu(b_z)
    return h @ w_proj


def encoder_layer(x, c, w_ada, b_ada, g_pre, w_in, conv_w, conv_b, conv_ln_g, conv_ln_b, conv_proj,
                  sgu_ln_g, sgu_ln_b, sgu_ws, sgu_bs, sgu_proj, w_out, g_post):
    mod = jax.nn.silu(c) @ w_ada + b_ada
    shift, scale, gate = jnp.split(mod[:, None, :], 3, axis=-1)
    h = rms_norm(x, g_pre) * (1 + scale) + shift
    p = h @ w_in
    a_val, a_glu, a_z, u, v, b_z, g_a, g_b = jnp.split(p, IN_SPLITS, axis=-1)
    y_a = conv_branch(a_val, a_glu, a_z, conv_w, conv_b, conv_ln_g, conv_ln_b, conv_proj)
    y_b = sgu_branch(u, v, b_z, sgu_ln_g, sgu_ln_b, sgu_ws, sgu_bs, sgu_proj)
    m = jax.nn.sigmoid(g_a) * y_a + jax.nn.sigmoid(g_b) * y_b
    o = m @ w_out
    return x + gate * rms_norm(o, g_post)


def trunk(x, c, w_ada, b_ada, g_pre, w_in, conv_w, conv_b, conv_ln_g, conv_ln_b, conv_proj,
          sgu_ln_g, sgu_ln_b, sgu_ws, sgu_bs, sgu_proj, w_out, g_post):
    for l in range(DEPTH):
        x = encoder_layer(x, c, w_ada[l], b_ada[l], g_pre[l], w_in[l], conv_w[l], conv_b[l],
                          conv_ln_g[l], conv_ln_b[l], conv_proj[l], sgu_ln_g[l], sgu_ln_b[l],
                          sgu_ws[l], sgu_bs[l], sgu_proj[l], w_out[l], g_post[l])
    return x


def setup_inputs(seed: int = 0) -> dict:
    key = jax.random.key(seed)
    ks = jax.random.split(key, 24)
    f32 = jnp.float32
    nrm = lambda k, shape, s: jax.random.normal(k, shape, f32) * s
    D = D_MODEL
    return {
        "x_prompt": nrm(ks[0], (BATCH, SEQ, D), 1.0),
        "x_sample": nrm(ks[1], (DEC_BATCH, DEC_SEQ, D), 1.0),
        "c_prompt": nrm(ks[2], (BATCH, D), 1.0),
        "c_sample": nrm(ks[3], (DEC_BATCH, D), 1.0),
        "w_ada": nrm(ks[4], (DEPTH, D, 3 * D), 0.5 * D ** -0.5),
        "b_ada": nrm(ks[5], (DEPTH, 3 * D), 0.02),
        "g_pre": 1.0 + nrm(ks[6], (DEPTH, D), 0.02),
        "w_in": nrm(ks[7], (DEPTH, D, IN_COLS), D ** -0.5),
        "conv_w": nrm(ks[8], (DEPTH, CONV_KERNEL, CONV_WIDTH), CONV_KERNEL ** -0.5),
        "conv_b": nrm(ks[9], (DEPTH, CONV_WIDTH), 0.02),
        "conv_ln_g": 1.0 + nrm(ks[10], (DEPTH, CONV_WIDTH), 0.02),
        "conv_ln_b": nrm(ks[11], (DEPTH, CONV_WIDTH), 0.02),
        "conv_proj": nrm(ks[12], (DEPTH, CONV_WIDTH, D), CONV_WIDTH ** -0.5),
        "sgu_ln_g": 1.0 + nrm(ks[13], (DEPTH, SGU_WIDTH), 0.02),
        "sgu_ln_b": nrm(ks[14], (DEPTH, SGU_WIDTH), 0.02),
        "sgu_ws": nrm(ks[15], (DEPTH, SGU_GROUPS, CHUNK, CHUNK), CHUNK ** -0.5),
        "sgu_bs": 1.0 + nrm(ks[16], (DEPTH, SGU_GROUPS, CHUNK), 0.02),
        "sgu_proj": nrm(ks[17], (DEPTH, SGU_WIDTH, D), SGU_WIDTH ** -0.5),
        "w_out": nrm(ks[18], (DEPTH, D, D), D ** -0.5),
        "g_post": 1.0 + nrm(ks[19], (DEPTH, D), 0.02),
    }


def reference(x_prompt, x_sample, c_prompt, c_sample, w_ada, b_ada, g_pre, w_in, conv_w, conv_b,
              conv_ln_g, conv_ln_b, conv_proj, sgu_ln_g, sgu_ln_b, sgu_ws, sgu_bs, sgu_proj, w_out, g_post):
    y_prompt = trunk(x_prompt, c_prompt, w_ada, b_ada, g_pre, w_in, conv_w, conv_b, conv_ln_g, conv_ln_b,
                     conv_proj, sgu_ln_g, sgu_ln_b, sgu_ws, sgu_bs, sgu_proj, w_out, g_post)
    y_sample = trunk(x_sample, c_sample, w_ada, b_ada, g_pre, w_in, conv_w, conv_b, conv_ln_g, conv_ln_b,
                     conv_proj, sgu_ln_g, sgu_ln_b, sgu_ws, sgu_bs, sgu_proj, w_out, g_post)
    return (y_prompt, y_sample)
```

```python
import contextlib
import numpy as np
import concourse.bass as bass
import concourse.mybir as mybir
from concourse.bass_utils import run_bass_kernel_spmd

F32 = mybir.dt.float32
BF16 = mybir.dt.bfloat16
AF = mybir.ActivationFunctionType
ALU = mybir.AluOpType

D = 1024
KC = 8
NT = 384
NTH = NT + 30
NGRP = 44
GW = 256
RING = 8
RMS_EPS = 1e-6
LN_EPS = 1e-5


class Cfg:
    def __init__(self, depth=4, T=(4096, 2048), debug=None):
        self.depth = depth
        self.T = T
        self.debug = debug


def _merge(d, s):
    for k, v in s.items():
        if d.get(k, 0) < v:
            d[k] = v


class Sem:
    def __init__(self, h):
        self.h = h


class Buf:
    def __init__(self, name=""):
        self.name = name
        self.w = {}
        self.r = {}


class Eng:
    def __init__(self, e, sem, is_pe=False):
        self.e = e
        self.sem = sem
        self.cnt = 0
        self.waited = {}
        self.is_pe = is_pe

    def _wait(self, deps):
        for s, v in deps.items():
            if self.waited.get(s, 0) < v:
                self.e.wait_ge(s.h, v)
                self.waited[s] = v

    def _deps(self, reads, writes):
        deps = {}
        for b in reads:
            _merge(deps, b.w)
        for b in writes:
            _merge(deps, b.w)
            _merge(deps, b.r)
        return deps

    def _compute_deps(self, reads, writes):
        deps = self._deps(reads, writes)
        if self.sem in deps:
            own = 0
            if not self.is_pe:
                for b in reads:
                    own = max(own, b.w.get(self.sem, 0))
            if own:
                deps[self.sem] = own
            else:
                del deps[self.sem]
        return deps

    def _commit(self, tok, reads, writes):
        for b in reads:
            _merge(b.r, tok)
        for b in writes:
            b.w = dict(tok)
            b.r = {}

    def op(self, fn, reads=(), writes=()):
        self._wait(self._compute_deps(reads, writes))
        ins = fn()
        self.cnt += 1
        ins.then_inc(self.sem.h, 1)
        self._commit({self.sem: self.cnt}, reads, writes)

    def group(self, fns, reads=(), writes=()):
        self._wait(self._compute_deps(reads, writes))
        ins = None
        for f in fns:
            ins = f()
        self.cnt += 1
        ins.then_inc(self.sem.h, 1)
        self._commit({self.sem: self.cnt}, reads, writes)


class DmaQ:
    def __init__(self, eng, sems):
        self.eng = eng
        self.sems = sems
        self.n = 0

    def dma(self, out, in_, reads=(), writes=()):
        deps = self.eng._deps(reads, writes)
        k = self.n % len(self.sems)
        sem = self.sems[k]
        prev = 16 * (self.n // len(self.sems))
        if prev:
            _merge(deps, {sem: prev})
        self.eng._wait(deps)
        self.eng.e.dma_start(out=out, in_=in_).then_inc(sem.h, 16)
        self.n += 1
        self.eng._commit({sem: prev + 16}, reads, writes)

    def final_tokens(self):
        toks = {}
        for k, sem in enumerate(self.sems):
            cnt = (self.n - k + len(self.sems) - 1) // len(self.sems) if self.n > k else 0
            if cnt:
                toks[sem] = 16 * cnt
        return toks


class Rot:
    def __init__(self, items):
        self.items = items
        self.i = 0

    def next(self):
        it = self.items[self.i % len(self.items)]
        self.i += 1
        return it


def build_nc(cfg):
    DEPTH = cfg.depth
    Tseg = cfg.T
    nc = bass.Bass("TRN2", target_bir_lowering=False)
    Ls = [[T + (DEPTH - l) * 128 for l in range(1, DEPTH + 1)] for T in Tseg]

    def dram(name, shape, dt, kind):
        return nc.dram_tensor(name, list(shape), dt, kind=kind).ap()

    xin = [dram(f"x{s}", [D, Ls[s][0] + 128], F32, "ExternalInput") for s in range(2)]
    yout = [dram(f"y{s}", [D, Tseg[s]], F32, "ExternalOutput") for s in range(2)]
    xscr = [[dram(f"xs{s}_{l}", [D, Ls[s][l - 1]], F32, "Internal") for l in range(1, DEPTH)] for s in range(2)]
    X = [[xin[s]] + xscr[s] + [yout[s]] for s in range(2)]
    Xv = [[a.rearrange("(kc p) t -> p kc t", p=128) for a in X[s]] for s in range(2)]
    Xlen = [[Ls[s][0] + 128] + [Ls[s][l - 1] for l in range(1, DEPTH)] + [Tseg[s]] for s in range(2)]

    w_ada = dram("w_ada", [DEPTH, D, 3 * D], F32, "ExternalInput")
    w_in = dram("w_in", [DEPTH, D, 8 * D], F32, "ExternalInput")
    conv_proj = dram("conv_proj", [DEPTH, D, D], F32, "ExternalInput")
    sgu_proj = dram("sgu_proj", [DEPTH, D, D], F32, "ExternalInput")
    w_out = dram("w_out", [DEPTH, D, D], F32, "ExternalInput")
    cvec_d = dram("cvec", [128, KC, 2], F32, "ExternalInput")
    bada_d = dram("bada", [128, DEPTH, 24], F32, "ExternalInput")
    pvec_d = dram("pvec", [128, DEPTH, 7, KC], F32, "ExternalInput")
    wcol_d = dram("wcol", [128, DEPTH, 256], F32, "ExternalInput")
    mask_d = dram("mask", [128, 32], F32, "ExternalInput")
    wst_d = dram("wst", [128, DEPTH, 8, 128], F32, "ExternalInput")
    bsb_d = dram("bsb", [128, DEPTH, 8, 128], F32, "ExternalInput")
    wb = dram("wb", [DEPTH, NGRP, 128, KC, GW], BF16, "Internal")
    wba = dram("wba", [DEPTH, 12, 128, KC, GW], BF16, "Internal")
    dbg_d = dram("dbg", [128, 3312], F32, "ExternalOutput") if cfg.debug else None

    def wsrc(l, gid):
        if gid < 32:
            src, c0 = w_in[l], gid * GW
        elif gid < 36:
            src, c0 = conv_proj[l], (gid - 32) * GW
        elif gid < 40:
            src, c0 = sgu_proj[l], (gid - 36) * GW
        else:
            src, c0 = w_out[l], (gid - 40) * GW
        return src.rearrange("(kc p) c -> p kc c", p=128)[:, :, c0:c0 + GW]

    with contextlib.ExitStack() as es:
        def sb(name, shape, dt):
            return es.enter_context(nc.sbuf_tensor("sb_" + name, list(shape), dt))

        def mksem(name):
            return Sem(es.enter_context(nc.semaphore(name)))

        pe = Eng(nc.tensor, mksem("s_pe"), is_pe=True)
        act = Eng(nc.scalar, mksem("s_act"))
        dve = Eng(nc.vector, mksem("s_dve"))
        pool = Eng(nc.gpsimd, mksem("s_pool"))
        sp = Eng(nc.sync, mksem("s_sp"))
        qs = DmaQ(sp, [mksem(f"s_qs{i}") for i in range(24)])
        qg = DmaQ(pool, [mksem(f"s_qg{i}") for i in range(8)])
        qa = DmaQ(act, [mksem(f"s_qa{i}") for i in range(16)])

        ring = [sb(f"ring{i}", [128, KC, GW], BF16) for i in range(RING)]
        ring_b = [Buf(f"ring{i}") for i in range(RING)]
        xt = sb("xt", [128, KC, NTH], F32); xt_b = Buf("xt")
        xsq = sb("xsq", [128, KC, NTH], BF16); xsq_b = Buf("xsq")
        h = sb("h", [128, KC, NTH], BF16); h_b = Buf("h")
        HP = sb("HP", [128, 4, KC, NTH], BF16); HP_blk = [[Buf() for _ in range(4)] for _ in range(4)]
        HP_all = [b for row in HP_blk for b in row]
        WQ = sb("WQ", [128, 256, 32], BF16); WQ_b = Buf("WQ")
        saz = sb("saz", [128, KC, NTH], BF16); saz_b = Buf("saz")
        ubz = sb("ubz", [128, KC, NTH], BF16); ubz_b = Buf("ubz")
        sga = sb("sga", [128, KC, NTH], BF16); sga_b = Buf("sga")
        sgb = sb("sgb", [128, KC, NTH], BF16); sgb_b = Buf("sgb")
        vn = sb("vn", [128, 3, 1104], BF16); vn_b = Buf("vn")
        hg = sb("hg", [128, KC, NTH], BF16); hgp_b = [Buf(f"hg{i}") for i in range(4)]
        sq2 = sb("sq2", [128, KC, NTH], BF16); sq2_b = Buf("sq2")
        bigA = sb("bigA", [128, 3072], F32); bigA_b = Buf("bigA")
        bigB = sb("bigB", [128, 3072], F32); bigB_b = Buf("bigB")
        sgt = [sb(f"sgt{i}", [128, NTH], F32) for i in range(2)]; sgt_b = [Buf(), Buf()]
        stmp = [sb(f"stmp{i}", [128, NT], BF16) for i in range(2)]; stmp_b = [Buf(), Buf()]
        mt = [sb(f"mt{i}", [128, NT], F32) for i in range(2)]; mt_b = [Buf(), Buf()]
        tt = [sb(f"tt{i}", [128, NT], F32) for i in range(2)]; tt_b = [Buf(), Buf()]
        lt = [sb(f"lt{i}", [128, NT], F32) for i in range(3)]; lt_b = [Buf(), Buf(), Buf()]
        pre_sq = sb("pre_sq", [128, NTH], F32); pre_b = Buf("pre")
        vst = sb("vst", [128, 3, 2, 6], F32); vmv = sb("vmv", [128, 3, 2], F32)
        vsq = sb("vsq", [128, 3], F32); vr = sb("vr", [128, 3], F32); vnm = sb("vnm", [128, 3], F32)
        vs_b = Buf("vstats")
        ones = sb("ones", [128, 128], BF16)
        c32 = sb("c32", [128, 128], BF16)
        part = [sb(f"part{i}", [128, NTH], BF16) for i in range(4)]; part_b = [Buf() for _ in range(4)]
        part_rot = [0]
        eps_rms = sb("eps_rms", [128, 1], F32)
        eps_ln = sb("eps_ln", [128, 1], F32)
        maskf = sb("maskf", [128, 32], F32)
        wcol = sb("wcol", [128, 256], F32); wcol_b = Buf("wcol")
        wst = sb("wst", [128, 8, 128], BF16); wst_b = Buf("wst")
        BB = sb("BB", [128, 8, 128], F32); BB_b = Buf("BB")
        cvec = sb("cvec", [128, KC, 2], F32)
        scv = sb("scv", [128, KC, 2], BF16)
        bada = sb("bada", [128, DEPTH, 24], F32)
        pvec = sb("pvec", [128, DEPTH, 7, KC], F32)
        modv = sb("modv", [128, DEPTH, 24, 2], F32)
        Aall = sb("Aall", [128, DEPTH, 2, KC], F32)
        Gall = sb("Gall", [128, DEPTH, 2, KC], F32)
        par_b = Buf("params")

        xcb = vn
        vnf = vn[:, :, :].rearrange("p a b -> p (a b)")

        def v8(flat, w):
            return flat[:, 0:8 * w].rearrange("p (c t) -> p c t", c=8)

        banks = [es.enter_context(nc.psum_tensor(f"bank{i}", [128, 512], F32)) for i in range(8)]
        bank_b = [Buf(f"bank{i}") for i in range(8)]
        PA = Rot([(banks[i], bank_b[i]) for i in range(4)])
        PB = Rot([(banks[i], bank_b[i]) for i in (4, 5)])
        S1, S1_b = banks[6], bank_b[6]
        S2, S2_b = banks[7], bank_b[7]

        dve.op(lambda: nc.vector.memset(ones[:, :], 1.0), writes=[par_b])
        dve.op(lambda: nc.vector.memset(c32[:, :], 1.0 / 32.0), writes=[par_b])
        dve.op(lambda: nc.vector.memset(eps_rms[:, :], RMS_EPS), writes=[par_b])
        dve.op(lambda: nc.vector.memset(eps_ln[:, :], LN_EPS), writes=[par_b])
        pool.op(lambda: nc.gpsimd.memset(HP[:, :, :, :], 0.0), writes=HP_all)
        qs.dma(maskf[:, :], mask_d, writes=[par_b])
        qs.dma(cvec[:, :, :], cvec_d, writes=[par_b])
        qs.dma(bada[:, :, :], bada_d, writes=[par_b])
        qs.dma(pvec[:, :, :, :], pvec_d, writes=[par_b])

        tiles = []
        for l in range(1, DEPTH + 1):
            for s in range(2):
                L = Ls[s][l - 1]
                nchunks = L // 128
                ntl_ = (nchunks + 2) // 3
                sizes = [3] * ntl_
                for k in range(3 * ntl_ - nchunks):
                    sizes[ntl_ - 1 - k] -= 1
                assert sum(sizes) == nchunks and min(sizes) >= 2
                t0 = 0
                for i, nc_ in enumerate(sizes):
                    ntc = 128 * nc_
                    thin = (l < DEPTH and i == len(sizes) - 1)
                    tiles.append(dict(l=l, s=s, i=i, t0=t0, nch=nc_, hl=(t0 > 0),
                                      hr=(t0 + ntc + 15 <= Xlen[s][l - 1]),
                                      ntc=(ntc - 112 if thin else ntc)))
                    t0 += ntc
        GORDER = [0, 4, 1, 5, 2, 6, 3, 7] + list(range(24, 32)) + list(range(16, 20)) + list(range(12, 16)) \
            + list(range(20, 24)) + list(range(8, 12)) + list(range(36, 40)) + list(range(32, 36)) + list(range(40, 44))
        uses = [("ada", 0, g) for g in range(12)]
        for ti, td in enumerate(tiles):
            if ti + 1 < len(tiles) and tiles[ti + 1]["l"] != td["l"]:
                uses += [("ada", tiles[ti + 1]["l"] - 1, g) for g in range(12)]
            for g in GORDER:
                uses.append(("w", td["l"] - 1, g))

        wb_b = [[Buf() for _ in range(NGRP)] for _ in range(DEPTH)]
        wba_b = [[Buf() for _ in range(12)] for _ in range(DEPTH)]
        conv_list = []
        for l in range(DEPTH):
            conv_list += [("ada", l, g) for g in range(12)] + [("w", l, g) for g in GORDER]
        conv_state = dict(n=0)

        def issue_conversions(upto_layer, count):
            n = 0
            while n < count and conv_state["n"] < len(conv_list) and conv_list[conv_state["n"]][1] <= upto_layer:
                kind, l, g = conv_list[conv_state["n"]]
                if kind == "ada":
                    src = w_ada[l].rearrange("(kc p) c -> p kc c", p=128)[:, :, g * GW:(g + 1) * GW]
                    qg.dma(wba[l, g], src, writes=[wba_b[l][g]])
                else:
                    qg.dma(wb[l, g], wsrc(l, g), writes=[wb_b[l][g]])
                conv_state["n"] += 1
                n += 1

        def layer_loads(l0):
            qs.dma(wcol[:, :], wcol_d[:, l0], writes=[wcol_b])
            qg.dma(wst[:, :, :], wst_d[:, l0], writes=[wst_b])
            qs.dma(BB[:, :, :], bsb_d[:, l0], writes=[BB_b])

        layer_loads(0)
        issue_conversions(0, 12 + NGRP)

        ring_state = dict(issued=0, used=0)

        def issue_loads(upto):
            while ring_state["issued"] < min(upto, len(uses)):
                m = ring_state["issued"]
                kind, l, g = uses[m]
                slot = m % RING
                if kind == "ada":
                    qs.dma(ring[slot][:, :, :], wba[l, g], reads=[wba_b[l][g]], writes=[ring_b[slot]])
                else:
                    qs.dma(ring[slot][:, :, :], wb[l, g], reads=[wb_b[l][g]], writes=[ring_b[slot]])
                ring_state["issued"] += 1

        def need_group(kind, l, g):
            m = ring_state["used"]
            assert uses[m] == (kind, l, g), (uses[m], kind, l, g)
            issue_loads(m + RING - 1)
            ring_state["used"] += 1
            slot = m % RING
            return ring[slot], ring_b[slot]

        act.op(lambda: nc.scalar.activation(out=scv[:, :, :], in_=cvec[:, :, :], func=AF.Silu),
               reads=[par_b], writes=[par_b])
        mod_b = [Buf(f"mod{l}") for l in range(DEPTH)]

        def emit_mod(l):
            for g in range(12):
                wt, wt_b = need_group("ada", l, g)
                fns = []
                for oc2 in range(2):
                    col = (g * 2 + oc2) * 2
                    for kc in range(KC):
                        fns.append(lambda wt=wt, kc=kc, oc2=oc2, col=col: nc.tensor.matmul(
                            S2[:, col:col + 2], wt[:, kc, oc2 * 128:(oc2 + 1) * 128], scv[:, kc, :],
                            start=(kc == 0), stop=(kc == KC - 1)))
                pe.group(fns, reads=[wt_b, par_b], writes=[S2_b])
            dve.op(lambda: nc.vector.tensor_tensor(
                out=modv[:, l, :, :], in0=S2[:, 0:48].rearrange("p (c s) -> p c s", s=2),
                in1=bada[:, l, :].unsqueeze(2).broadcast_to([128, 24, 2]), op=ALU.add),
                reads=[S2_b, par_b], writes=[mod_b[l]])
            for s in range(2):
                dve.op(lambda s=s: nc.vector.scalar_tensor_tensor(
                    out=Aall[:, l, s, :], in0=modv[:, l, 8:16, s], scalar=1.0, in1=pvec[:, l, 0, :],
                    op0=ALU.add, op1=ALU.mult), reads=[par_b, mod_b[l]], writes=[mod_b[l]])
                dve.op(lambda s=s: nc.vector.tensor_tensor(
                    out=Gall[:, l, s, :], in0=modv[:, l, 16:24, s], in1=pvec[:, l, 6, :], op=ALU.mult),
                    reads=[par_b, mod_b[l]], writes=[mod_b[l]])


        def layer_setup(l0):
            if l0 > 0:
                layer_loads(l0)
            dve.op(lambda: nc.vector.tensor_tensor(
                out=WQ[:, :, :], in0=maskf[:, :].unsqueeze(1).broadcast_to([128, 256, 32]),
                in1=wcol[:, :].unsqueeze(2).broadcast_to([128, 256, 32]), op=ALU.mult),
                reads=[par_b, wcol_b], writes=[WQ_b])
            for half in range(2):
                fns = []
                for gg in range(4):
                    g = half * 4 + gg
                    fns.append(lambda g=g, gg=gg: nc.tensor.matmul(
                        S2[:, gg * 128:(gg + 1) * 128], ones[:, :], wst[:, g, :], start=True, stop=True))
                pe.group(fns, reads=[par_b, wst_b], writes=[S2_b])
                for gg in range(4):
                    g = half * 4 + gg
                    dve.op(lambda g=g, gg=gg: nc.vector.scalar_tensor_tensor(
                        out=BB[:, g, :], in0=S2[:, gg * 128:(gg + 1) * 128], scalar=pvec[:, l0, 5, g:g + 1],
                        in1=BB[:, g, :], op0=ALU.mult, op1=ALU.add), reads=[S2_b, par_b, BB_b], writes=[BB_b])

        xd_b = {}

        def xbuf(s, l, i):
            return xd_b.setdefault((s, l, i), Buf())

        def xrange_bufs(s, l, a, b):
            if l == 0:
                return []
            return [xbuf(s, l, c) for c in range(a // 128, (b + 127) // 128)]

        def ntiles(s, l):
            return (Xlen[s][l] + NT - 1) // NT

        def chan_sum_a(src_of_kc, n, rd_bufs):
            bq, bq_b = PA.next()
            fns = []
            for q in range(2):
                for j in range(4):
                    fns.append(lambda q=q, j=j: nc.tensor.matmul(
                        bq[32 * j:32 * j + 32, 0:n], ones[:, 0:32], src_of_kc(4 * q + j),
                        start=(q == 0), stop=(q == 1), tile_position=(0, 32 * j)))
            pe.group(fns, reads=rd_bufs + [par_b], writes=[bq_b])
            k = part_rot[0] % len(part)
            part_rot[0] += 1
            act.op(lambda: nc.scalar.activation(out=part[k][:, 0:n], in_=bq[:, 0:n], func=AF.Copy),
                   reads=[bq_b], writes=[part_b[k]])
            return (k, n)

        def chan_sum_b(tok, out_ap, out_buf):
            k, n = tok
            pe.group([lambda: nc.tensor.matmul(out_ap, c32[:, :], part[k][:, 0:n], start=True, stop=True)],
                     reads=[part_b[k], par_b], writes=[out_buf])

        def pre_load(td):
            l, s, t0, nch = td["l"], td["s"], td["t0"], td["nch"]
            ntc = 128 * nch
            nth = ntc + 30
            lo = 0 if td["hl"] else 15
            hi = nth if td["hr"] else 15 + ntc
            if not td["hl"]:
                dve.op(lambda: nc.vector.memset(xt[:, :, 0:15], 0.0), writes=[xt_b])
            if not td["hr"]:
                dve.op(lambda: nc.vector.memset(xt[:, :, 15 + ntc:nth], 0.0), writes=[xt_b])
            rd = xrange_bufs(s, l - 1, t0 - 15 + lo, t0 - 15 + hi)
            qs.dma(xt[:, :, lo:hi], Xv[s][l - 1][:, :, t0 - 15 + lo:t0 - 15 + hi], reads=rd, writes=[xt_b])

        def pre_square(td):
            nth = 128 * td["nch"] + 30
            act.op(lambda: nc.scalar.activation(out=xsq[:, :, 0:nth], in_=xt[:, :, 0:nth], func=AF.Square),
                   reads=[xt_b], writes=[xsq_b])

        def pre_stats_a(td):
            nth = 128 * td["nch"] + 30
            td["_pre_tok"] = chan_sum_a(lambda kc: xsq[:, kc, 0:nth], nth, [xsq_b])

        def pre_stats_b(td):
            nth = 128 * td["nch"] + 30
            chan_sum_b(td["_pre_tok"], S1[:, 0:nth], S1_b)

        def pre_sqrt(td):
            nth = 128 * td["nch"] + 30
            act.op(lambda: nc.scalar.activation(out=pre_sq[:, 0:nth], in_=S1[:, 0:nth], func=AF.Ln,
                                                bias=eps_rms[:, 0:1], scale=1.0 / D), reads=[S1_b, par_b], writes=[pre_b])
            act.op(lambda: nc.scalar.activation(out=pre_sq[:, 0:nth], in_=pre_sq[:, 0:nth], func=AF.Exp,
                                                scale=-0.5), reads=[pre_b], writes=[pre_b])

        def pre_scale(td):
            nth = 128 * td["nch"] + 30
            dve.op(lambda: nc.vector.tensor_tensor(
                out=xt[:, :, 0:nth], in0=xt[:, :, 0:nth],
                in1=pre_sq[:, 0:nth].unsqueeze(1).broadcast_to([128, KC, nth]), op=ALU.mult),
                reads=[xt_b, pre_b], writes=[xt_b])

        def pre_affine(td):
            nth = 128 * td["nch"] + 30
            l0, s = td["l"] - 1, td["s"]
            for kc in range(KC):
                act.op(lambda kc=kc: nc.scalar.activation(
                    out=h[:, kc, 0:nth], in_=xt[:, kc, 0:nth], func=AF.Identity,
                    scale=Aall[:, l0, s, kc:kc + 1], bias=modv[:, l0, kc, s:s + 1]),
                    reads=[xt_b, par_b, mod_b[l0]], writes=[h_b])

        def dump(name, td, ap, buf):
            if cfg.debug == name and td["l"] == 1 and td["s"] == 0 and td["i"] == 0:
                n = 1
                for d_ in ap.shape[1:]:
                    n *= d_
                dst = dbg_d[:, 0:n]
                if len(ap.shape) == 3:
                    dst = dst.rearrange("p (a b) -> p a b", a=ap.shape[1])
                qg.dma(dst, ap, reads=[buf], writes=[Buf()])

        def tile_body(td, nxt, prev_tail):
            l, s, t0, nch = td["l"], td["s"], td["t0"], td["nch"]
            l0 = l - 1
            ntc = td["ntc"]
            thin = ntc != 128 * nch
            nth = ntc + 30
            M0, M1 = 15, 15 + ntc

            def mm_chunk(wt, wcol0, rhs_of_kc, out_ap):
                return [lambda kc=kc: nc.tensor.matmul(out_ap, wt[:, kc, wcol0:wcol0 + 128], rhs_of_kc(kc),
                                                       start=(kc == 0), stop=(kc == KC - 1)) for kc in range(KC)]

            def proj_like(g0, rhs_of_kc, rd_bufs, evac):
                for q in range(4):
                    wt, wt_b = need_group("w", l0, g0 + q)
                    for jj in range(2):
                        oc = q * 2 + jj
                        bk, bk_b = PA.next()
                        pe.group(mm_chunk(wt, jj * 128, rhs_of_kc, bk[:, 0:ntc]), reads=[wt_b] + rd_bufs, writes=[bk_b])
                        evac(oc, bk, bk_b)

            hglu = hg
            for pair in range(4):
                wv, wv_b = need_group("w", l0, pair)
                wg, wg_b = need_group("w", l0, 4 + pair)
                for jj in range(2):
                    j = pair * 2 + jj
                    bv, bv_b = PA.next()
                    bg, bg_b = PA.next()
                    pe.group(mm_chunk(wv, jj * 128, lambda kc: h[:, kc, 0:nth], bv[:, 0:nth]),
                             reads=[wv_b, h_b], writes=[bv_b])
                    pe.group(mm_chunk(wg, jj * 128, lambda kc: h[:, kc, 0:nth], bg[:, 0:nth]),
                             reads=[wg_b, h_b], writes=[bg_b])
                    k = j % 2
                    act.op(lambda bg=bg, k=k: nc.scalar.activation(out=sgt[k][:, 0:nth], in_=bg[:, 0:nth],
                                                                    func=AF.Sigmoid),
                           reads=[bg_b], writes=[sgt_b[k]])
                    dve.op(lambda bv=bv, k=k, j=j: nc.vector.tensor_tensor(
                        out=hglu[:, j, 0:nth], in0=bv[:, 0:nth], in1=sgt[k][:, 0:nth], op=ALU.mult),
                        reads=[bv_b, sgt_b[k]], writes=[hgp_b[pair]])
                if prev_tail is not None and pair == 0:
                    prev_tail[0]()
                if prev_tail is not None and pair == 1:
                    prev_tail[1]()
                c0 = 2 * pair
                if not td["hl"]:
                    dve.op(lambda c0=c0: nc.vector.memset(hglu[:, c0:c0 + 2, 0:15], 0.0), writes=[hgp_b[pair]])
                if not td["hr"]:
                    dve.op(lambda c0=c0: nc.vector.memset(hglu[:, c0:c0 + 2, M1:nth], 0.0), writes=[hgp_b[pair]])
            issue_loads(ring_state["used"] + RING)
            hflat = hglu[:, :, :].rearrange("p c t -> p (c t)")

            def rep_one(q, sft, j):
                q.dma(HP[32 * sft:32 * sft + 32, j, :, :].rearrange("p c t -> p (c t)")[:, 0:KC * NTH - sft],
                      hflat[32 * j:32 * j + 32, sft:KC * NTH], reads=hgp_b, writes=[HP_blk[sft][j]])

            def replicate():
                for j in range(4):
                    rep_one(qs, 0, j)
                issue_loads(ring_state["used"] + RING)
                for j in range(4):
                    rep_one(qs, 1, j)
            act_jobs = [(sft, j) for sft in (2, 3) for j in range(4)]
            if prev_tail is not None:
                prev_tail[2]()
            dump("h", td, h[:, :, 0:nth], h_b)
            dump("hglu", td, hglu[:, :, 0:nth], hgp_b[3])
            if nxt is not None:
                pre_square(nxt)
            for g0, dst, dst_b in ((24, sga, sga_b), (28, sgb, sgb_b)):
                def evac_gate(oc, bk, bk_b, dst=dst, dst_b=dst_b):
                    act.op(lambda: nc.scalar.activation(out=dst[:, oc, 0:ntc], in_=bk[:, 0:ntc], func=AF.Sigmoid),
                           reads=[bk_b], writes=[dst_b])
                    if g0 == 24 and act_jobs:
                        rep_one(qa, *act_jobs.pop(0))
                proj_like(g0, lambda kc: h[:, kc, M0:M1], [h_b], evac_gate)
                if g0 == 24:
                    replicate()
            vgf = bigB
            vbanks = {}
            for q in range(4):
                wt, wt_b = need_group("w", l0, 16 + q)
                half = q // 2
                for tc in range(nch):
                    if q % 2 == 0:
                        vbanks[tc] = PA.next()
                    bk, bk_b = vbanks[tc]
                    fns = [lambda kc=kc, tc=tc, bk=bk, wt=wt, q=q: nc.tensor.matmul(
                        bk[:, (q % 2) * GW:(q % 2 + 1) * GW], h[:, kc, M0 + tc * 128:M0 + (tc + 1) * 128], wt[:, kc, :],
                        start=(kc == 0), stop=(kc == KC - 1)) for kc in range(KC)]
                    pe.group(fns, reads=[wt_b, h_b], writes=[bk_b])
                    if q % 2 == 1:
                        act.op(lambda tc=tc, bk=bk, half=half: nc.scalar.activation(
                            out=vgf[:, tc * 1024 + half * 512:tc * 1024 + (half + 1) * 512], in_=bk[:, :],
                            func=AF.Gelu_apprx_tanh), reads=[bk_b], writes=[bigB_b])
                        dve.op(lambda tc=tc, half=half: nc.vector.bn_stats(
                            out=vst[:, tc, half, :], in_=vgf[:, tc * 1024 + half * 512:tc * 1024 + (half + 1) * 512]),
                            reads=[bigB_b], writes=[vs_b])
            for tc in range(nch):
                dve.op(lambda tc=tc: nc.vector.bn_aggr(out=vmv[:, tc, :],
                                                        in_=vst[:, tc, :, :].rearrange("p a b -> p (a b)")),
                       reads=[vs_b], writes=[vs_b])
            xc = v8(bigA, ntc)
            xcbv = v8(vnf, ntc)
            sq2v = sq2
            for c in range(KC):
                bk, bk_b = PB.next()
                fns = []
                for m in range(8):
                    for j in range(4):
                        idx = (c * 4 + j) * 8 + m
                        fns.append(lambda m=m, j=j, idx=idx, bk=bk, c=c: nc.tensor.matmul(
                            bk[32 * j:32 * j + 32, 0:ntc], WQ[:, idx, :], HP[:, j, c, 4 * m:4 * m + ntc],
                            start=(m == 0), stop=(m == 7), tile_position=(0, 32 * j)))
                pe.group(fns, reads=[WQ_b] + HP_all, writes=[bk_b])
                act.op(lambda bk=bk, c=c: nc.scalar.activation(
                    out=xc[:, c, :], in_=bk[:, 0:ntc], func=AF.Identity, bias=pvec[:, l0, 1, c:c + 1], scale=1.0),
                    reads=[bk_b, par_b], writes=[bigA_b])
                act.op(lambda bk=bk, c=c: nc.scalar.activation(
                    out=sq2v[:, c, 0:ntc], in_=bk[:, 0:ntc], func=AF.Square, bias=pvec[:, l0, 1, c:c + 1], scale=1.0),
                    reads=[bk_b, par_b], writes=[sq2_b])
                dve.op(lambda c=c: nc.vector.tensor_copy(out=xcbv[:, c, :], in_=xc[:, c, :]),
                       reads=[bigA_b], writes=[vn_b])
            dump("xc", td, xc, bigA_b)
            if nxt is not None:
                pre_stats_a(nxt)
            tok_sum = chan_sum_a(lambda kc: xcbv[:, kc, :], ntc, [vn_b])
            tok_sq = chan_sum_a(lambda kc: sq2v[:, kc, 0:ntc], ntc, [sq2_b])

            def stats_b():
                if nxt is not None:
                    pre_stats_b(nxt)
                bkm, bkm_b = PA.next()
                chan_sum_b(tok_sum, bkm[:, 0:ntc], bkm_b)
                dve.op(lambda: nc.vector.tensor_scalar(out=lt[0][:, 0:ntc], in0=bkm[:, 0:ntc], scalar1=1.0 / D,
                                                       scalar2=None, op0=ALU.mult), reads=[bkm_b], writes=[lt_b[0]])
                dve.op(lambda: nc.vector.tensor_tensor(out=lt[1][:, 0:ntc], in0=lt[0][:, 0:ntc], in1=lt[0][:, 0:ntc],
                                                       op=ALU.mult), reads=[lt_b[0]], writes=[lt_b[1]])
                chan_sum_b(tok_sq, S2[:, 0:ntc], S2_b)
                dve.op(lambda: nc.vector.scalar_tensor_tensor(
                    out=lt[1][:, 0:ntc], in0=S2[:, 0:ntc], scalar=1.0 / D, in1=lt[1][:, 0:ntc],
                    op0=ALU.mult, op1=ALU.subtract), reads=[S2_b, lt_b[1]], writes=[lt_b[1]])
                act.op(lambda: nc.scalar.activation(out=lt[2][:, 0:ntc], in_=lt[1][:, 0:ntc], func=AF.Ln,
                                                    bias=eps_ln[:, 0:1], scale=1.0), reads=[lt_b[1], par_b], writes=[lt_b[2]])
                act.op(lambda: nc.scalar.activation(out=lt[2][:, 0:ntc], in_=lt[2][:, 0:ntc], func=AF.Exp,
                                                    scale=-0.5), reads=[lt_b[2]], writes=[lt_b[2]])
                act.op(lambda: nc.scalar.activation(out=vsq[:, 0:nch], in_=vmv[:, 0:nch, 1], func=AF.Ln,
                                                    bias=eps_ln[:, 0:1], scale=1.0), reads=[vs_b, par_b], writes=[vs_b])
                act.op(lambda: nc.scalar.activation(out=vr[:, 0:nch], in_=vsq[:, 0:nch], func=AF.Exp,
                                                    scale=-0.5), reads=[vs_b], writes=[vs_b])
                if nxt is not None:
                    pre_sqrt(nxt)
                dve.op(lambda: nc.vector.scalar_tensor_tensor(
                    out=vnm[:, 0:nch], in0=vmv[:, 0:nch, 0], scalar=-1.0, in1=vr[:, 0:nch], op0=ALU.mult, op1=ALU.mult),
                    reads=[vs_b], writes=[vs_b])
                for tc in range(nch):
                    dve.op(lambda tc=tc: nc.vector.tensor_scalar(
                        out=vn[:, tc, 0:1024], in0=vgf[:, tc * 1024:(tc + 1) * 1024], scalar1=vr[:, tc:tc + 1],
                        scalar2=vnm[:, tc:tc + 1], op0=ALU.mult, op1=ALU.add),
                        reads=[bigB_b, vs_b], writes=[vn_b])
                dve.op(lambda: nc.vector.scalar_tensor_tensor(
                    out=lt[0][:, 0:ntc], in0=lt[0][:, 0:ntc], scalar=-1.0, in1=lt[2][:, 0:ntc],
                    op0=ALU.mult, op1=ALU.mult), reads=[lt_b[0], lt_b[2]], writes=[lt_b[0]])
                dump("vn", td, vn[:, 0:nch, 0:1024], vn_b)

            def evac_u(oc, bk, bk_b):
                act.op(lambda: nc.scalar.activation(out=ubz[:, oc, 0:ntc], in_=bk[:, 0:ntc], func=AF.Gelu_apprx_tanh),
                       reads=[bk_b], writes=[ubz_b])
            stats_b()
            proj_like(12, lambda kc: h[:, kc, M0:M1], [h_b], evac_u)
            def evac_bz(oc, bk, bk_b):
                k = oc % 2
                act.op(lambda: nc.scalar.activation(out=stmp[k][:, 0:ntc], in_=bk[:, 0:ntc], func=AF.Silu),
                       reads=[bk_b], writes=[stmp_b[k]])
                dve.op(lambda: nc.vector.tensor_tensor(
                    out=ubz[:, oc, 0:ntc], in0=ubz[:, oc, 0:ntc], in1=stmp[k][:, 0:ntc], op=ALU.mult),
                    reads=[ubz_b, stmp_b[k]], writes=[ubz_b])
            hb = hg

            def evac_bz_sgu(g, bk, bk_b):
                evac_bz(g, bk, bk_b)
                b2, b2_b = PB.next()
                wd = [min(128, ntc - tc * 128) for tc in range(nch)]
                fns = [lambda tc=tc: nc.tensor.matmul(
                    b2[:, tc * 128:tc * 128 + wd[tc]], vn[:, tc, g * 128:(g + 1) * 128], wst[:, g, 0:wd[tc]],
                    start=True, stop=True) for tc in range(nch)]
                pe.group(fns, reads=[vn_b, wst_b], writes=[b2_b])
                k = g % 2
                nfull = nch - 1 if thin else nch
                dve.op(lambda: nc.vector.scalar_tensor_tensor(
                    out=mt[k][:, 0:nfull * 128].rearrange("p (a b) -> p a b", b=128),
                    in0=b2[:, 0:nfull * 128].rearrange("p (a b) -> p a b", b=128), scalar=pvec[:, l0, 4, g:g + 1],
                    in1=BB[:, g, :].unsqueeze(1).broadcast_to([128, nfull, 128]), op0=ALU.mult, op1=ALU.add),
                    reads=[b2_b, BB_b, par_b], writes=[mt_b[k]])
                if thin:
                    dve.op(lambda: nc.vector.scalar_tensor_tensor(
                        out=mt[k][:, nfull * 128:ntc], in0=b2[:, nfull * 128:ntc], scalar=pvec[:, l0, 4, g:g + 1],
                        in1=BB[:, g, 0:ntc - nfull * 128], op0=ALU.mult, op1=ALU.add),
                        reads=[b2_b, BB_b, par_b, mt_b[k]], writes=[mt_b[k]])
                dve.op(lambda: nc.vector.tensor_tensor(
                    out=hb[:, g, 0:ntc], in0=mt[k][:, 0:ntc], in1=ubz[:, g, 0:ntc], op=ALU.mult),
                    reads=[mt_b[k], ubz_b], writes=[hgp_b[g // 2]])
            proj_like(20, lambda kc: h[:, kc, M0:M1], [h_b], evac_bz_sgu)
            dve.op(lambda: nc.vector.tensor_tensor(
                out=xc, in0=xc, in1=lt[2][:, 0:ntc].unsqueeze(1).broadcast_to([128, KC, ntc]), op=ALU.mult),
                reads=[bigA_b, lt_b[2]], writes=[bigA_b])
            dve.op(lambda: nc.vector.tensor_tensor(
                out=xc, in0=xc, in1=lt[0][:, 0:ntc].unsqueeze(1).broadcast_to([128, KC, ntc]), op=ALU.add),
                reads=[bigA_b, lt_b[0]], writes=[bigA_b])
            if nxt is not None:
                pre_scale(nxt)
            proj_like(8, lambda kc: h[:, kc, M0:M1], [h_b],
                      lambda oc, bk, bk_b: act.op(
                          lambda: nc.scalar.activation(out=saz[:, oc, 0:ntc], in_=bk[:, 0:ntc], func=AF.Silu),
                          reads=[bk_b], writes=[saz_b]))
            dump("hb", td, hb[:, :, 0:ntc], hgp_b[3])
            dump("saz", td, saz[:, :, 0:ntc], saz_b)
            dump("sga", td, sga[:, :, 0:ntc], sga_b)
            ha = ubz
            for c in range(KC):
                act.op(lambda c=c: nc.scalar.activation(
                    out=ha[:, c, 0:ntc], in_=xc[:, c, :], func=AF.Silu, scale=pvec[:, l0, 2, c:c + 1],
                    bias=pvec[:, l0, 3, c:c + 1]), reads=[bigA_b, par_b], writes=[ubz_b])
            if nxt is not None:
                pre_affine(nxt)
            tb = v8(bigB, ntc)
            def evac_tb(oc, bk, bk_b):
                dve.op(lambda: nc.vector.tensor_tensor(
                    out=tb[:, oc, :], in0=bk[:, 0:ntc], in1=sgb[:, oc, 0:ntc], op=ALU.mult),
                    reads=[bk_b, sgb_b], writes=[bigB_b])
                dve.op(lambda: nc.vector.tensor_tensor(
                    out=ha[:, oc, 0:ntc], in0=ha[:, oc, 0:ntc], in1=saz[:, oc, 0:ntc], op=ALU.mult),
                    reads=[ubz_b, saz_b], writes=[ubz_b])
            proj_like(36, lambda kc: hb[:, kc, 0:ntc], hgp_b, evac_tb)
            dump("ha", td, ha[:, :, 0:ntc], ubz_b)
            mv = v8(vnf, ntc)
            mch_b = [Buf() for _ in range(KC)]

            def evac_m(oc, bk, bk_b):
                k = oc % 2
                dve.op(lambda: nc.vector.tensor_tensor(
                    out=tt[k][:, 0:ntc], in0=bk[:, 0:ntc], in1=sga[:, oc, 0:ntc], op=ALU.mult),
                    reads=[bk_b, sga_b], writes=[tt_b[k]])
                dve.op(lambda: nc.vector.tensor_tensor(
                    out=mv[:, oc, :], in0=tt[k][:, 0:ntc], in1=tb[:, oc, :], op=ALU.add),
                    reads=[tt_b[k], bigB_b], writes=[vn_b, mch_b[oc]])
            proj_like(32, lambda kc: ha[:, kc, 0:ntc], [ubz_b], evac_m)
            dump("m", td, mv, vn_b)
            xr = v8(bigB, ntc)
            rd = xrange_bufs(s, l - 1, t0, t0 + ntc)
            qs.dma(xr, Xv[s][l - 1][:, :, t0:t0 + ntc], reads=rd, writes=[bigB_b])
            of = v8(bigA, ntc)

            def evac_o(oc, bk, bk_b):
                act.op(lambda: nc.scalar.activation(out=sq2v[:, oc, 0:ntc], in_=bk[:, 0:ntc], func=AF.Square),
                       reads=[bk_b], writes=[sq2_b])
                act.op(lambda: nc.scalar.activation(out=of[:, oc, :], in_=bk[:, 0:ntc], func=AF.Identity,
                                                    scale=Gall[:, l0, s, oc:oc + 1]),
                       reads=[bk_b, par_b, mod_b[l0]], writes=[bigA_b])
            for q in range(4):
                wt, wt_b = need_group("w", l0, 40 + q)
                for jj in range(2):
                    oc = q * 2 + jj
                    bk, bk_b = PA.next()
                    fns = mm_chunk(wt, jj * 128, lambda kc: mv[:, kc, :], bk[:, 0:ntc])
                    if oc == 0:
                        pe.group(fns[:KC - 1], reads=[wt_b] + mch_b[:KC - 1], writes=[bk_b])
                        _merge(vn_b.r, {pe.sem: pe.cnt})
                        pe.group(fns[KC - 1:], reads=[wt_b, vn_b], writes=[bk_b])
                    else:
                        pe.group(fns, reads=[wt_b, vn_b], writes=[bk_b])
                    evac_o(oc, bk, bk_b)
            dump("of", td, of, bigA_b)
            post_tok = []

            def post_a():
                post_tok.append(chan_sum_a(lambda kc: sq2v[:, kc, 0:ntc], ntc, [sq2_b]))

            def post_b():
                chan_sum_b(post_tok[0], S2[:, 0:ntc], S2_b)
                act.op(lambda: nc.scalar.activation(out=lt[2][:, 0:ntc], in_=S2[:, 0:ntc], func=AF.Ln,
                                                    bias=eps_rms[:, 0:1], scale=1.0 / D),
                       reads=[S2_b, par_b], writes=[lt_b[2]])
                act.op(lambda: nc.scalar.activation(out=lt[2][:, 0:ntc], in_=lt[2][:, 0:ntc], func=AF.Exp,
                                                    scale=-0.5), reads=[lt_b[2]], writes=[lt_b[2]])

            def tail():
                dve.op(lambda: nc.vector.tensor_tensor(
                    out=of, in0=of, in1=lt[2][:, 0:ntc].unsqueeze(1).broadcast_to([128, KC, ntc]), op=ALU.mult),
                    reads=[bigA_b, lt_b[2]], writes=[bigA_b])
                dve.op(lambda: nc.vector.tensor_tensor(out=of, in0=of, in1=xr, op=ALU.add),
                       reads=[bigA_b, bigB_b], writes=[bigA_b])
                nst = min(ntc, Xlen[s][l] - t0)
                if nst > 0:
                    qg.dma(Xv[s][l][:, :, t0:t0 + nst], of[:, :, 0:nst], reads=[bigA_b],
                           writes=xrange_bufs(s, l, t0, t0 + nst))
            return (post_a, post_b, tail)

        cur_layer = 0
        first = tiles[0]
        pre_load(first)
        pre_square(first)
        pre_stats_a(first)
        pre_stats_b(first)
        pre_sqrt(first)
        pre_scale(first)
        emit_mod(0)
        pre_affine(first)
        ntl = [sum(1 for t in tiles if t["l"] == l) for l in range(1, DEPTH + 1)]
        prev_tail = None
        for ti, td in enumerate(tiles):
            if td["l"] != cur_layer:
                cur_layer = td["l"]
                layer_setup(cur_layer - 1)
            nxt = tiles[ti + 1] if ti + 1 < len(tiles) else None
            if nxt is not None and nxt["l"] != td["l"]:
                emit_mod(nxt["l"] - 1)
            if nxt is not None:
                pre_load(nxt)
            prev_tail = tile_body(td, nxt, prev_tail)
            per = (NGRP + 12 + ntl[cur_layer - 1] - 1) // ntl[cur_layer - 1] + 1
            issue_conversions(cur_layer, per)
        for f in prev_tail:
            f()
        fin = {}
        _merge(fin, qg.final_tokens())
        _merge(fin, qs.final_tokens())
        _merge(fin, qa.final_tokens())
        sp._wait(fin)
        pool._wait(dict(fin))
    return nc


def _prep_shared(inp, depth):
    f = np.float32
    sh = {}
    for k in ("w_ada", "w_in", "conv_proj", "sgu_proj", "w_out"):
        sh[k] = np.ascontiguousarray(inp[k][:depth], dtype=f)
    sh["bada"] = np.ascontiguousarray(inp["b_ada"][:depth].reshape(depth, 24, 128).transpose(2, 0, 1), dtype=f)
    pv = np.stack([inp[k][:depth] for k in ("g_pre", "conv_b", "conv_ln_g", "conv_ln_b", "sgu_ln_g", "sgu_ln_b",
                                            "g_post")], axis=1)
    sh["pvec"] = np.ascontiguousarray(pv.reshape(depth, 7, 8, 128).transpose(3, 0, 1, 2), dtype=f)
    m = np.zeros((128, 32), f)
    for s in range(4):
        m[32 * s + np.arange(32), np.arange(32)] = 1.0
    sh["mask"] = m
    return sh


def _prep_flip(inp, depth, flip):
    f = np.float32
    out = {}
    cw = inp["conv_w"][:depth]
    if flip:
        cw = cw[:, ::-1, :]
    cwp = np.zeros((depth, 32, 1024), f)
    cwp[:, :31] = cw
    a = cwp.reshape(depth, 8, 4, 8, 4, 32)
    out["wcol"] = np.ascontiguousarray(a.transpose(2, 5, 0, 3, 4, 1).reshape(128, depth, 256), dtype=f)
    ws = inp["sgu_ws"][:depth]
    bs = inp["sgu_bs"][:depth]
    if flip:
        ws = ws[:, :, ::-1, ::-1]
        bs = bs[:, :, ::-1]
    out["wst"] = np.ascontiguousarray(ws.transpose(3, 0, 1, 2), dtype=f)
    out["bsb"] = np.ascontiguousarray(np.broadcast_to(bs[None], (128,) + bs.shape), dtype=f)
    return out


def run(inp, cfg, seqs):
    depth = cfg.depth
    TS, TP = cfg.T
    nc = build_nc(cfg)
    shared = _prep_shared(inp, depth)
    flips = [_prep_flip(inp, depth, False), _prep_flip(inp, depth, True)]
    in_maps = []
    for (si, pi) in seqs:
        for half in range(2):
            mp = dict(shared)
            mp.update(flips[half])
            xs = []
            for (x, T) in ((inp["x_sample"][si], TS), (inp["x_prompt"][pi], TP)):
                if half == 1:
                    x = x[::-1]
                L1 = T + depth * 128
                xs.append(np.ascontiguousarray(x[:L1].T, dtype=np.float32))
            mp["x0"], mp["x1"] = xs
            c2 = np.stack([inp["c_sample"][si], inp["c_prompt"][pi]], axis=-1)
            mp["cvec"] = np.ascontiguousarray(c2.reshape(8, 128, 2).transpose(1, 0, 2), dtype=np.float32)
            in_maps.append(mp)
    res = run_bass_kernel_spmd(nc, in_maps, core_ids=list(range(len(in_maps))))
    if cfg.debug:
        run.dbg = [r["dbg"] for r in res.results]
    ys = np.zeros((len(seqs), 2 * TS, D), np.float32)
    yp = np.zeros((len(seqs), 2 * TP, D), np.float32)
    for n, (si, pi) in enumerate(seqs):
        for half in range(2):
            r = res.results[2 * n + half]
            a = r["y0"].T
            b = r["y1"].T
            if half == 0:
                ys[n, :TS] = a
                yp[n, :TP] = b
            else:
                ys[n, TS:] = a[::-1]
                yp[n, TP:] = b[::-1]
    return yp, ys


def kernel(**inputs):
    inp = {k: np.asarray(v) for k, v in inputs.items()}
    cfg = Cfg(depth=4, T=(4096, 2048))
    yp, ys = run(inp, cfg, [(i, i) for i in range(4)])
    return (yp.astype(np.float32), ys.astype(np.float32))
```

```python
import contextlib
import numpy as np
import concourse.bass as bass
import concourse.mybir as mybir
from concourse.bass_utils import run_bass_kernel_spmd

F32 = mybir.dt.float32
BF16 = mybir.dt.bfloat16
AF = mybir.ActivationFunctionType
ALU = mybir.AluOpType

D = 1024
KC = 8
NT = 384
NTH = NT + 30
NGRP = 44
GW = 256
RING = 8
RMS_EPS = 1e-6
LN_EPS = 1e-5


class Cfg:
    def __init__(self, depth=4, T=(4096, 2048), debug=None):
        self.depth = depth
        self.T = T
        self.debug = debug


def _merge(d, s):
    for k, v in s.items():
        if d.get(k, 0) < v:
            d[k] = v


class Sem:
    def __init__(self, h):
        self.h = h


class Buf:
    def __init__(self, name=""):
        self.name = name
        self.w = {}
        self.r = {}


class Eng:
    def __init__(self, e, sem, is_pe=False):
        self.e = e
        self.sem = sem
        self.cnt = 0
        self.waited = {}
        self.is_pe = is_pe

    def _wait(self, deps):
        for s, v in deps.items():
            if self.waited.get(s, 0) < v:
                self.e.wait_ge(s.h, v)
                self.waited[s] = v

    def _deps(self, reads, writes):
        deps = {}
        for b in reads:
            _merge(deps, b.w)
        for b in writes:
            _merge(deps, b.w)
            _merge(deps, b.r)
        return deps

    def _compute_deps(self, reads, writes):
        deps = self._deps(reads, writes)
        if self.sem in deps:
            own = 0
            if not self.is_pe:
                for b in reads:
                    own = max(own, b.w.get(self.sem, 0))
            if own:
                deps[self.sem] = own
            else:
                del deps[self.sem]
        return deps

    def _commit(self, tok, reads, writes):
        for b in reads:
            _merge(b.r, tok)
        for b in writes:
            b.w = dict(tok)
            b.r = {}

    def op(self, fn, reads=(), writes=()):
        self._wait(self._compute_deps(reads, writes))
        ins = fn()
        self.cnt += 1
        ins.then_inc(self.sem.h, 1)
        self._commit({self.sem: self.cnt}, reads, writes)

    def group(self, fns, reads=(), writes=()):
        self._wait(self._compute_deps(reads, writes))
        ins = None
        for f in fns:
            ins = f()
        self.cnt += 1
        ins.then_inc(self.sem.h, 1)
        self._commit({self.sem: self.cnt}, reads, writes)


class DmaQ:
    def __init__(self, eng, sems):
        self.eng = eng
        self.sems = sems
        self.n = 0

    def dma(self, out, in_, reads=(), writes=()):
        deps = self.eng._deps(reads, writes)
        k = self.n % len(self.sems)
        sem = self.sems[k]
        prev = 16 * (self.n // len(self.sems))
        if prev:
            _merge(deps, {sem: prev})
        self.eng._wait(deps)
        self.eng.e.dma_start(out=out, in_=in_).then_inc(sem.h, 16)
        self.n += 1
        self.eng._commit({sem: prev + 16}, reads, writes)

    def final_tokens(self):
        toks = {}
        for k, sem in enumerate(self.sems):
            cnt = (self.n - k + len(self.sems) - 1) // len(self.sems) if self.n > k else 0
            if cnt:
                toks[sem] = 16 * cnt
        return toks


class Rot:
    def __init__(self, items):
        self.items = items
        self.i = 0

    def next(self):
        it = self.items[self.i % len(self.items)]
        self.i += 1
        return it


def build_nc(cfg):
    DEPTH = cfg.depth
    Tseg = cfg.T
    nc = bass.Bass("TRN2", target_bir_lowering=False)
    Ls = [[T + (DEPTH - l) * 128 for l in range(1, DEPTH + 1)] for T in Tseg]

    def dram(name, shape, dt, kind):
        return nc.dram_tensor(name, list(shape), dt, kind=kind).ap()

    xin = [dram(f"x{s}", [D, Ls[s][0] + 128], F32, "ExternalInput") for s in range(2)]
    yout = [dram(f"y{s}", [D, Tseg[s]], F32, "ExternalOutput") for s in range(2)]
    xscr = [[dram(f"xs{s}_{l}", [D, Ls[s][l - 1]], F32, "Internal") for l in range(1, DEPTH)] for s in range(2)]
    X = [[xin[s]] + xscr[s] + [yout[s]] for s in range(2)]
    Xv = [[a.rearrange("(kc p) t -> p kc t", p=128) for a in X[s]] for s in range(2)]
    Xlen = [[Ls[s][0] + 128] + [Ls[s][l - 1] for l in range(1, DEPTH)] + [Tseg[s]] for s in range(2)]

    w_ada = dram("w_ada", [DEPTH, D, 3 * D], F32, "ExternalInput")
    w_in = dram("w_in", [DEPTH, D, 8 * D], F32, "ExternalInput")
    conv_proj = dram("conv_proj", [DEPTH, D, D], F32, "ExternalInput")
    sgu_proj = dram("sgu_proj", [DEPTH, D, D], F32, "ExternalInput")
    w_out = dram("w_out", [DEPTH, D, D], F32, "ExternalInput")
    cvec_d = dram("cvec", [128, KC, 2], F32, "ExternalInput")
    bada_d = dram("bada", [128, DEPTH, 24], F32, "ExternalInput")
    pvec_d = dram("pvec", [128, DEPTH, 7, KC], F32, "ExternalInput")
    wcol_d = dram("wcol", [128, DEPTH, 256], F32, "ExternalInput")
    mask_d = dram("mask", [128, 32], F32, "ExternalInput")
    wst_d = dram("wst", [128, DEPTH, 8, 128], F32, "ExternalInput")
    bsb_d = dram("bsb", [128, DEPTH, 8, 128], F32, "ExternalInput")
    wb = dram("wb", [DEPTH, NGRP, 128, KC, GW], BF16, "Internal")
    wba = dram("wba", [DEPTH, 12, 128, KC, GW], BF16, "Internal")
    dbg_d = dram("dbg", [128, 3312], F32, "ExternalOutput") if cfg.debug else None

    def wsrc(l, gid):
        if gid < 32:
            src, c0 = w_in[l], gid * GW
        elif gid < 36:
            src, c0 = conv_proj[l], (gid - 32) * GW
        elif gid < 40:
            src, c0 = sgu_proj[l], (gid - 36) * GW
        else:
            src, c0 = w_out[l], (gid - 40) * GW
        return src.rearrange("(kc p) c -> p kc c", p=128)[:, :, c0:c0 + GW]

    with contextlib.ExitStack() as es:
        def sb(name, shape, dt):
            return es.enter_context(nc.sbuf_tensor("sb_" + name, list(shape), dt))

        def mksem(name):
            return Sem(es.enter_context(nc.semaphore(name)))

        pe = Eng(nc.tensor, mksem("s_pe"), is_pe=True)
        act = Eng(nc.scalar, mksem("s_act"))
        dve = Eng(nc.vector, mksem("s_dve"))
        pool = Eng(nc.gpsimd, mksem("s_pool"))
        sp = Eng(nc.sync, mksem("s_sp"))
        qs = DmaQ(sp, [mksem(f"s_qs{i}") for i in range(24)])
        qg = DmaQ(pool, [mksem(f"s_qg{i}") for i in range(8)])
        qa = DmaQ(act, [mksem(f"s_qa{i}") for i in range(16)])

        ring = [sb(f"ring{i}", [128, KC, GW], BF16) for i in range(RING)]
        ring_b = [Buf(f"ring{i}") for i in range(RING)]
        xt = sb("xt", [128, KC, NTH], F32); xt_b = Buf("xt")
        xsq = sb("xsq", [128, KC, NTH], BF16); xsq_b = Buf("xsq")
        h = sb("h", [128, KC, NTH], BF16); h_b = Buf("h")
        HP = sb("HP", [128, 4, KC, NTH], BF16); HP_blk = [[Buf() for _ in range(4)] for _ in range(4)]
        HP_all = [b for row in HP_blk for b in row]
        WQ = sb("WQ", [128, 256, 32], BF16); WQ_b = Buf("WQ")
        saz = sb("saz", [128, KC, NTH], BF16); saz_b = Buf("saz")
        ubz = sb("ubz", [128, KC, NTH], BF16); ubz_b = Buf("ubz")
        sga = sb("sga", [128, KC, NTH], BF16); sga_b = Buf("sga")
        sgb = sb("sgb", [128, KC, NTH], BF16); sgb_b = Buf("sgb")
        vn = sb("vn", [128, 3, 1104], BF16); vn_b = Buf("vn")
        hg = sb("hg", [128, KC, NTH], BF16); hgp_b = [Buf(f"hg{i}") for i in range(4)]
        sq2 = sb("sq2", [128, KC, NTH], BF16); sq2_b = Buf("sq2")
        bigA = sb("bigA", [128, 3072], F32); bigA_b = Buf("bigA")
        bigB = sb("bigB", [128, 3072], F32); bigB_b = Buf("bigB")
        sgt = [sb(f"sgt{i}", [128, NTH], F32) for i in range(2)]; sgt_b = [Buf(), Buf()]
        stmp = [sb(f"stmp{i}", [128, NT], BF16) for i in range(2)]; stmp_b = [Buf(), Buf()]
        mt = [sb(f"mt{i}", [128, NT], F32) for i in range(2)]; mt_b = [Buf(), Buf()]
        tt = [sb(f"tt{i}", [128, NT], F32) for i in range(2)]; tt_b = [Buf(), Buf()]
        lt = [sb(f"lt{i}", [128, NT], F32) for i in range(3)]; lt_b = [Buf(), Buf(), Buf()]
        pre_sq = sb("pre_sq", [128, NTH], F32); pre_b = Buf("pre")
        vst = sb("vst", [128, 3, 2, 6], F32); vmv = sb("vmv", [128, 3, 2], F32)
        vsq = sb("vsq", [128, 3], F32); vr = sb("vr", [128, 3], F32); vnm = sb("vnm", [128, 3], F32)
        vs_b = Buf("vstats")
        ones = sb("ones", [128, 128], BF16)
        c32 = sb("c32", [128, 128], BF16)
        part = [sb(f"part{i}", [128, NTH], BF16) for i in range(4)]; part_b = [Buf() for _ in range(4)]
        part_rot = [0]
        eps_rms = sb("eps_rms", [128, 1], F32)
        eps_ln = sb("eps_ln", [128, 1], F32)
        maskf = sb("maskf", [128, 32], F32)
        wcol = sb("wcol", [128, 256], F32); wcol_b = Buf("wcol")
        wst = sb("wst", [128, 8, 128], BF16); wst_b = Buf("wst")
        BB = sb("BB", [128, 8, 128], F32); BB_b = Buf("BB")
        cvec = sb("cvec", [128, KC, 2], F32)
        scv = sb("scv", [128, KC, 2], BF16)
        bada = sb("bada", [128, DEPTH, 24], F32)
        pvec = sb("pvec", [128, DEPTH, 7, KC], F32)
        modv = sb("modv", [128, DEPTH, 24, 2], F32)
        Aall = sb("Aall", [128, DEPTH, 2, KC], F32)
        Gall = sb("Gall", [128, DEPTH, 2, KC], F32)
        par_b = Buf("params")

        xcb = vn
        vnf = vn[:, :, :].rearrange("p a b -> p (a b)")

        def v8(flat, w):
            return flat[:, 0:8 * w].rearrange("p (c t) -> p c t", c=8)

        banks = [es.enter_context(nc.psum_tensor(f"bank{i}", [128, 512], F32)) for i in range(8)]
        bank_b = [Buf(f"bank{i}") for i in range(8)]
        PA = Rot([(banks[i], bank_b[i]) for i in range(4)])
        PB = Rot([(banks[i], bank_b[i]) for i in (4, 5)])
        S1, S1_b = banks[6], bank_b[6]
        S2, S2_b = banks[7], bank_b[7]

        dve.op(lambda: nc.vector.memset(ones[:, :], 1.0), writes=[par_b])
        dve.op(lambda: nc.vector.memset(c32[:, :], 1.0 / 32.0), writes=[par_b])
        dve.op(lambda: nc.vector.memset(eps_rms[:, :], RMS_EPS), writes=[par_b])
        dve.op(lambda: nc.vector.memset(eps_ln[:, :], LN_EPS), writes=[par_b])
        pool.op(lambda: nc.gpsimd.memset(HP[:, :, :, :], 0.0), writes=HP_all)
        qs.dma(maskf[:, :], mask_d, writes=[par_b])
        qs.dma(cvec[:, :, :], cvec_d, writes=[par_b])
        qs.dma(bada[:, :, :], bada_d, writes=[par_b])
        qs.dma(pvec[:, :, :, :], pvec_d, writes=[par_b])

        tiles = []
        for l in range(1, DEPTH + 1):
            for s in range(2):
                L = Ls[s][l - 1]
                nchunks = L // 128
                ntl_ = (nchunks + 2) // 3
                sizes = [3] * ntl_
                for k in range(3 * ntl_ - nchunks):
                    sizes[ntl_ - 1 - k] -= 1
                assert sum(sizes) == nchunks and min(sizes) >= 2
                t0 = 0
                for i, nc_ in enumerate(sizes):
                    ntc = 128 * nc_
                    thin = (l < DEPTH and i == len(sizes) - 1)
                    tiles.append(dict(l=l, s=s, i=i, t0=t0, nch=nc_, hl=(t0 > 0),
                                      hr=(t0 + ntc + 15 <= Xlen[s][l - 1]),
                                      ntc=(ntc - 112 if thin else ntc)))
                    t0 += ntc
        GORDER = [0, 4, 1, 5, 2, 6, 3, 7] + list(range(24, 32)) + list(range(16, 20)) + list(range(12, 16)) \
            + list(range(20, 24)) + list(range(8, 12)) + list(range(36, 40)) + list(range(32, 36)) + list(range(40, 44))
        uses = [("ada", 0, g) for g in range(12)]
        for ti, td in enumerate(tiles):
            if ti + 1 < len(tiles) and tiles[ti + 1]["l"] != td["l"]:
                uses += [("ada", tiles[ti + 1]["l"] - 1, g) for g in range(12)]
            for g in GORDER:
                uses.append(("w", td["l"] - 1, g))

        wb_b = [[Buf() for _ in range(NGRP)] for _ in range(DEPTH)]
        wba_b = [[Buf() for _ in range(12)] for _ in range(DEPTH)]
        conv_list = []
        for l in range(DEPTH):
            conv_list += [("ada", l, g) for g in range(12)] + [("w", l, g) for g in GORDER]
        conv_state = dict(n=0)

        def issue_conversions(upto_layer, count):
            n = 0
            while n < count and conv_state["n"] < len(conv_list) and conv_list[conv_state["n"]][1] <= upto_layer:
                kind, l, g = conv_list[conv_state["n"]]
                if kind == "ada":
                    src = w_ada[l].rearrange("(kc p) c -> p kc c", p=128)[:, :, g * GW:(g + 1) * GW]
                    qg.dma(wba[l, g], src, writes=[wba_b[l][g]])
                else:
                    qg.dma(wb[l, g], wsrc(l, g), writes=[wb_b[l][g]])
                conv_state["n"] += 1
                n += 1

        def layer_loads(l0):
            qs.dma(wcol[:, :], wcol_d[:, l0], writes=[wcol_b])
            qg.dma(wst[:, :, :], wst_d[:, l0], writes=[wst_b])
            qs.dma(BB[:, :, :], bsb_d[:, l0], writes=[BB_b])

        layer_loads(0)
        issue_conversions(0, 12 + NGRP)

        ring_state = dict(issued=0, used=0)

        def issue_loads(upto):
            while ring_state["issued"] < min(upto, len(uses)):
                m = ring_state["issued"]
                kind, l, g = uses[m]
                slot = m % RING
                if kind == "ada":
                    qs.dma(ring[slot][:, :, :], wba[l, g], reads=[wba_b[l][g]], writes=[ring_b[slot]])
                else:
                    qs.dma(ring[slot][:, :, :], wb[l, g], reads=[wb_b[l][g]], writes=[ring_b[slot]])
                ring_state["issued"] += 1

        def need_group(kind, l, g):
            m = ring_state["used"]
            assert uses[m] == (kind, l, g), (uses[m], kind, l, g)
            issue_loads(m + RING - 1)
            ring_state["used"] += 1
            slot = m % RING
            return ring[slot], ring_b[slot]

        act.op(lambda: nc.scalar.activation(out=scv[:, :, :], in_=cvec[:, :, :], func=AF.Silu),
               reads=[par_b], writes=[par_b])
        mod_b = [Buf(f"mod{l}") for l in range(DEPTH)]

        def emit_mod(l):
            for g in range(12):
                wt, wt_b = need_group("ada", l, g)
                fns = []
                for oc2 in range(2):
                    col = (g * 2 + oc2) * 2
                    for kc in range(KC):
                        fns.append(lambda wt=wt, kc=kc, oc2=oc2, col=col: nc.tensor.matmul(
                            S2[:, col:col + 2], wt[:, kc, oc2 * 128:(oc2 + 1) * 128], scv[:, kc, :],
                            start=(kc == 0), stop=(kc == KC - 1)))
                pe.group(fns, reads=[wt_b, par_b], writes=[S2_b])
            dve.op(lambda: nc.vector.tensor_tensor(
                out=modv[:, l, :, :], in0=S2[:, 0:48].rearrange("p (c s) -> p c s", s=2),
                in1=bada[:, l, :].unsqueeze(2).broadcast_to([128, 24, 2]), op=ALU.add),
                reads=[S2_b, par_b], writes=[mod_b[l]])
            for s in range(2):
                dve.op(lambda s=s: nc.vector.scalar_tensor_tensor(
                    out=Aall[:, l, s, :], in0=modv[:, l, 8:16, s], scalar=1.0, in1=pvec[:, l, 0, :],
                    op0=ALU.add, op1=ALU.mult), reads=[par_b, mod_b[l]], writes=[mod_b[l]])
                dve.op(lambda s=s: nc.vector.tensor_tensor(
                    out=Gall[:, l, s, :], in0=modv[:, l, 16:24, s], in1=pvec[:, l, 6, :], op=ALU.mult),
                    reads=[par_b, mod_b[l]], writes=[mod_b[l]])


        def layer_setup(l0):
            if l0 > 0:
                layer_loads(l0)
            dve.op(lambda: nc.vector.tensor_tensor(
                out=WQ[:, :, :], in0=maskf[:, :].unsqueeze(1).broadcast_to([128, 256, 32]),
                in1=wcol[:, :].unsqueeze(2).broadcast_to([128, 256, 32]), op=ALU.mult),
                reads=[par_b, wcol_b], writes=[WQ_b])
            for half in range(2):
                fns = []
                for gg in range(4):
                    g = half * 4 + gg
                    fns.append(lambda g=g, gg=gg: nc.tensor.matmul(
                        S2[:, gg * 128:(gg + 1) * 128], ones[:, :], wst[:, g, :], start=True, stop=True))
                pe.group(fns, reads=[par_b, wst_b], writes=[S2_b])
                for gg in range(4):
                    g = half * 4 + gg
                    dve.op(lambda g=g, gg=gg: nc.vector.scalar_tensor_tensor(
                        out=BB[:, g, :], in0=S2[:, gg * 128:(gg + 1) * 128], scalar=pvec[:, l0, 5, g:g + 1],
                        in1=BB[:, g, :], op0=ALU.mult, op1=ALU.add), reads=[S2_b, par_b, BB_b], writes=[BB_b])

        xd_b = {}

        def xbuf(s, l, i):
            return xd_b.setdefault((s, l, i), Buf())

        def xrange_bufs(s, l, a, b):
            if l == 0:
                return []
            return [xbuf(s, l, c) for c in range(a // 128, (b + 127) // 128)]

        def ntiles(s, l):
            return (Xlen[s][l] + NT - 1) // NT

        def chan_sum_a(src_of_kc, n, rd_bufs):
            bq, bq_b = PA.next()
            fns = []
            for q in range(2):
                for j in range(4):
                    fns.append(lambda q=q, j=j: nc.tensor.matmul(
                        bq[32 * j:32 * j + 32, 0:n], ones[:, 0:32], src_of_kc(4 * q + j),
                        start=(q == 0), stop=(q == 1), tile_position=(0, 32 * j)))
            pe.group(fns, reads=rd_bufs + [par_b], writes=[bq_b])
            k = part_rot[0] % len(part)
            part_rot[0] += 1
            act.op(lambda: nc.scalar.activation(out=part[k][:, 0:n], in_=bq[:, 0:n], func=AF.Copy),
                   reads=[bq_b], writes=[part_b[k]])
            return (k, n)

        def chan_sum_b(tok, out_ap, out_buf):
            k, n = tok
            pe.group([lambda: nc.tensor.matmul(out_ap, c32[:, :], part[k][:, 0:n], start=True, stop=True)],
                     reads=[part_b[k], par_b], writes=[out_buf])

        def pre_load(td):
            l, s, t0, nch = td["l"], td["s"], td["t0"], td["nch"]
            ntc = 128 * nch
            nth = ntc + 30
            lo = 0 if td["hl"] else 15
            hi = nth if td["hr"] else 15 + ntc
            if not td["hl"]:
                dve.op(lambda: nc.vector.memset(xt[:, :, 0:15], 0.0), writes=[xt_b])
            if not td["hr"]:
                dve.op(lambda: nc.vector.memset(xt[:, :, 15 + ntc:nth], 0.0), writes=[xt_b])
            rd = xrange_bufs(s, l - 1, t0 - 15 + lo, t0 - 15 + hi)
            qs.dma(xt[:, :, lo:hi], Xv[s][l - 1][:, :, t0 - 15 + lo:t0 - 15 + hi], reads=rd, writes=[xt_b])

        def pre_square(td):
            nth = 128 * td["nch"] + 30
            act.op(lambda: nc.scalar.activation(out=xsq[:, :, 0:nth], in_=xt[:, :, 0:nth], func=AF.Square),
                   reads=[xt_b], writes=[xsq_b])

        def pre_stats_a(td):
            nth = 128 * td["nch"] + 30
            td["_pre_tok"] = chan_sum_a(lambda kc: xsq[:, kc, 0:nth], nth, [xsq_b])

        def pre_stats_b(td):
            nth = 128 * td["nch"] + 30
            chan_sum_b(td["_pre_tok"], S1[:, 0:nth], S1_b)

        def pre_sqrt(td):
            nth = 128 * td["nch"] + 30
            act.op(lambda: nc.scalar.activation(out=pre_sq[:, 0:nth], in_=S1[:, 0:nth], func=AF.Ln,
                                                bias=eps_rms[:, 0:1], scale=1.0 / D), reads=[S1_b, par_b], writes=[pre_b])
            act.op(lambda: nc.scalar.activation(out=pre_sq[:, 0:nth], in_=pre_sq[:, 0:nth], func=AF.Exp,
                                                scale=-0.5), reads=[pre_b], writes=[pre_b])

        def pre_scale(td):
            nth = 128 * td["nch"] + 30
            dve.op(lambda: nc.vector.tensor_tensor(
                out=xt[:, :, 0:nth], in0=xt[:, :, 0:nth],
                in1=pre_sq[:, 0:nth].unsqueeze(1).broadcast_to([128, KC, nth]), op=ALU.mult),
                reads=[xt_b, pre_b], writes=[xt_b])

        def pre_affine(td):
            nth = 128 * td["nch"] + 30
            l0, s = td["l"] - 1, td["s"]
            for kc in range(KC):
                act.op(lambda kc=kc: nc.scalar.activation(
                    out=h[:, kc, 0:nth], in_=xt[:, kc, 0:nth], func=AF.Identity,
                    scale=Aall[:, l0, s, kc:kc + 1], bias=modv[:, l0, kc, s:s + 1]),
                    reads=[xt_b, par_b, mod_b[l0]], writes=[h_b])

        def dump(name, td, ap, buf):
            if cfg.debug == name and td["l"] == 1 and td["s"] == 0 and td["i"] == 0:
                n = 1
                for d_ in ap.shape[1:]:
                    n *= d_
                dst = dbg_d[:, 0:n]
                if len(ap.shape) == 3:
                    dst = dst.rearrange("p (a b) -> p a b", a=ap.shape[1])
                qg.dma(dst, ap, reads=[buf], writes=[Buf()])

        def tile_body(td, nxt, prev_tail):
            l, s, t0, nch = td["l"], td["s"], td["t0"], td["nch"]
            l0 = l - 1
            ntc = td["ntc"]
            thin = ntc != 128 * nch
            nth = ntc + 30
            M0, M1 = 15, 15 + ntc

            def mm_chunk(wt, wcol0, rhs_of_kc, out_ap):
                return [lambda kc=kc: nc.tensor.matmul(out_ap, wt[:, kc, wcol0:wcol0 + 128], rhs_of_kc(kc),
                                                       start=(kc == 0), stop=(kc == KC - 1)) for kc in range(KC)]

            def proj_like(g0, rhs_of_kc, rd_bufs, evac):
                for q in range(4):
                    wt, wt_b = need_group("w", l0, g0 + q)
                    for jj in range(2):
                        oc = q * 2 + jj
                        bk, bk_b = PA.next()
                        pe.group(mm_chunk(wt, jj * 128, rhs_of_kc, bk[:, 0:ntc]), reads=[wt_b] + rd_bufs, writes=[bk_b])
                        evac(oc, bk, bk_b)

            hglu = hg
            for pair in range(4):
                wv, wv_b = need_group("w", l0, pair)
                wg, wg_b = need_group("w", l0, 4 + pair)
                for jj in range(2):
                    j = pair * 2 + jj
                    bv, bv_b = PA.next()
                    bg, bg_b = PA.next()
                    pe.group(mm_chunk(wv, jj * 128, lambda kc: h[:, kc, 0:nth], bv[:, 0:nth]),
                             reads=[wv_b, h_b], writes=[bv_b])
                    pe.group(mm_chunk(wg, jj * 128, lambda kc: h[:, kc, 0:nth], bg[:, 0:nth]),
                             reads=[wg_b, h_b], writes=[bg_b])
                    k = j % 2
                    act.op(lambda bg=bg, k=k: nc.scalar.activation(out=sgt[k][:, 0:nth], in_=bg[:, 0:nth],
                                                                    func=AF.Sigmoid),
                           reads=[bg_b], writes=[sgt_b[k]])
                    dve.op(lambda bv=bv, k=k, j=j: nc.vector.tensor_tensor(
                        out=hglu[:, j, 0:nth], in0=bv[:, 0:nth], in1=sgt[k][:, 0:nth], op=ALU.mult),
                        reads=[bv_b, sgt_b[k]], writes=[hgp_b[pair]])
                if prev_tail is not None and pair == 0:
                    prev_tail[0]()
                if prev_tail is not None and pair == 1:
                    prev_tail[1]()
                c0 = 2 * pair
                if not td["hl"]:
                    dve.op(lambda c0=c0: nc.vector.memset(hglu[:, c0:c0 + 2, 0:15], 0.0), writes=[hgp_b[pair]])
                if not td["hr"]:
                    dve.op(lambda c0=c0: nc.vector.memset(hglu[:, c0:c0 + 2, M1:nth], 0.0), writes=[hgp_b[pair]])
            issue_loads(ring_state["used"] + RING)
            hflat = hglu[:, :, :].rearrange("p c t -> p (c t)")

            def rep_one(q, sft, j):
                q.dma(HP[32 * sft:32 * sft + 32, j, :, :].rearrange("p c t -> p (c t)")[:, 0:KC * NTH - sft],
                      hflat[32 * j:32 * j + 32, sft:KC * NTH], reads=hgp_b, writes=[HP_blk[sft][j]])

            def replicate():
                for j in range(4):
                    rep_one(qs, 0, j)
                issue_loads(ring_state["used"] + RING)
                for j in range(4):
                    rep_one(qs, 1, j)
            act_jobs = [(sft, j) for sft in (2, 3) for j in range(4)]
            if prev_tail is not None:
                prev_tail[2]()
            dump("h", td, h[:, :, 0:nth], h_b)
            dump("hglu", td, hglu[:, :, 0:nth], hgp_b[3])
            if nxt is not None:
                pre_square(nxt)
            for g0, dst, dst_b in ((24, sga, sga_b), (28, sgb, sgb_b)):
                def evac_gate(oc, bk, bk_b, dst=dst, dst_b=dst_b):
                    act.op(lambda: nc.scalar.activation(out=dst[:, oc, 0:ntc], in_=bk[:, 0:ntc], func=AF.Sigmoid),
                           reads=[bk_b], writes=[dst_b])
                    if g0 == 24 and act_jobs:
                        rep_one(qa, *act_jobs.pop(0))
                proj_like(g0, lambda kc: h[:, kc, M0:M1], [h_b], evac_gate)
                if g0 == 24:
                    replicate()
            vgf = bigB
            vbanks = {}
            for q in range(4):
                wt, wt_b = need_group("w", l0, 16 + q)
                half = q // 2
                for tc in range(nch):
                    if q % 2 == 0:
                        vbanks[tc] = PA.next()
                    bk, bk_b = vbanks[tc]
                    fns = [lambda kc=kc, tc=tc, bk=bk, wt=wt, q=q: nc.tensor.matmul(
                        bk[:, (q % 2) * GW:(q % 2 + 1) * GW], h[:, kc, M0 + tc * 128:M0 + (tc + 1) * 128], wt[:, kc, :],
                        start=(kc == 0), stop=(kc == KC - 1)) for kc in range(KC)]
                    pe.group(fns, reads=[wt_b, h_b], writes=[bk_b])
                    if q % 2 == 1:
                        act.op(lambda tc=tc, bk=bk, half=half: nc.scalar.activation(
                            out=vgf[:, tc * 1024 + half * 512:tc * 1024 + (half + 1) * 512], in_=bk[:, :],
                            func=AF.Gelu_apprx_tanh), reads=[bk_b], writes=[bigB_b])
                        dve.op(lambda tc=tc, half=half: nc.vector.bn_stats(
                            out=vst[:, tc, half, :], in_=vgf[:, tc * 1024 + half * 512:tc * 1024 + (half + 1) * 512]),
                            reads=[bigB_b], writes=[vs_b])
            for tc in range(nch):
                dve.op(lambda tc=tc: nc.vector.bn_aggr(out=vmv[:, tc, :],
                                                        in_=vst[:, tc, :, :].rearrange("p a b -> p (a b)")),
                       reads=[vs_b], writes=[vs_b])
            xc = v8(bigA, ntc)
            xcbv = v8(vnf, ntc)
            sq2v = sq2
            for c in range(KC):
                bk, bk_b = PB.next()
                fns = []
                for m in range(8):
                    for j in range(4):
                        idx = (c * 4 + j) * 8 + m
                        fns.append(lambda m=m, j=j, idx=idx, bk=bk, c=c: nc.tensor.matmul(
                            bk[32 * j:32 * j + 32, 0:ntc], WQ[:, idx, :], HP[:, j, c, 4 * m:4 * m + ntc],
                            start=(m == 0), stop=(m == 7), tile_position=(0, 32 * j)))
                pe.group(fns, reads=[WQ_b] + HP_all, writes=[bk_b])
                act.op(lambda bk=bk, c=c: nc.scalar.activation(
                    out=xc[:, c, :], in_=bk[:, 0:ntc], func=AF.Identity, bias=pvec[:, l0, 1, c:c + 1], scale=1.0),
                    reads=[bk_b, par_b], writes=[bigA_b])
                act.op(lambda bk=bk, c=c: nc.scalar.activation(
                    out=sq2v[:, c, 0:ntc], in_=bk[:, 0:ntc], func=AF.Square, bias=pvec[:, l0, 1, c:c + 1], scale=1.0),
                    reads=[bk_b, par_b], writes=[sq2_b])
                dve.op(lambda c=c: nc.vector.tensor_copy(out=xcbv[:, c, :], in_=xc[:, c, :]),
                       reads=[bigA_b], writes=[vn_b])
            dump("xc", td, xc, bigA_b)
            if nxt is not None:
                pre_stats_a(nxt)
            tok_sum = chan_sum_a(lambda kc: xcbv[:, kc, :], ntc, [vn_b])
            tok_sq = chan_sum_a(lambda kc: sq2v[:, kc, 0:ntc], ntc, [sq2_b])

            def stats_b():
                if nxt is not None:
                    pre_stats_b(nxt)
                bkm, bkm_b = PA.next()
                chan_sum_b(tok_sum, bkm[:, 0:ntc], bkm_b)
                dve.op(lambda: nc.vector.tensor_scalar(out=lt[0][:, 0:ntc], in0=bkm[:, 0:ntc], scalar1=1.0 / D,
                                                       scalar2=None, op0=ALU.mult), reads=[bkm_b], writes=[lt_b[0]])
                dve.op(lambda: nc.vector.tensor_tensor(out=lt[1][:, 0:ntc], in0=lt[0][:, 0:ntc], in1=lt[0][:, 0:ntc],
                                                       op=ALU.mult), reads=[lt_b[0]], writes=[lt_b[1]])
                chan_sum_b(tok_sq, S2[:, 0:ntc], S2_b)
                dve.op(lambda: nc.vector.scalar_tensor_tensor(
                    out=lt[1][:, 0:ntc], in0=S2[:, 0:ntc], scalar=1.0 / D, in1=lt[1][:, 0:ntc],
                    op0=ALU.mult, op1=ALU.subtract), reads=[S2_b, lt_b[1]], writes=[lt_b[1]])
                act.op(lambda: nc.scalar.activation(out=lt[2][:, 0:ntc], in_=lt[1][:, 0:ntc], func=AF.Ln,
                                                    bias=eps_ln[:, 0:1], scale=1.0), reads=[lt_b[1], par_b], writes=[lt_b[2]])
                act.op(lambda: nc.scalar.activation(out=lt[2][:, 0:ntc], in_=lt[2][:, 0:ntc], func=AF.Exp,
                                                    scale=-0.5), reads=[lt_b[2]], writes=[lt_b[2]])
                act.op(lambda: nc.scalar.activation(out=vsq[:, 0:nch], in_=vmv[:, 0:nch, 1], func=AF.Ln,
                                                    bias=eps_ln[:, 0:1], scale=1.0), reads=[vs_b, par_b], writes=[vs_b])
                act.op(lambda: nc.scalar.activation(out=vr[:, 0:nch], in_=vsq[:, 0:nch], func=AF.Exp,
                                                    scale=-0.5), reads=[vs_b], writes=[vs_b])
                if nxt is not None:
                    pre_sqrt(nxt)
                dve.op(lambda: nc.vector.scalar_tensor_tensor(
                    out=vnm[:, 0:nch], in0=vmv[:, 0:nch, 0], scalar=-1.0, in1=vr[:, 0:nch], op0=ALU.mult, op1=ALU.mult),
                    reads=[vs_b], writes=[vs_b])
                for tc in range(nch):
                    dve.op(lambda tc=tc: nc.vector.tensor_scalar(
                        out=vn[:, tc, 0:1024], in0=vgf[:, tc * 1024:(tc + 1) * 1024], scalar1=vr[:, tc:tc + 1],
                        scalar2=vnm[:, tc:tc + 1], op0=ALU.mult, op1=ALU.add),
                        reads=[bigB_b, vs_b], writes=[vn_b])
                dve.op(lambda: nc.vector.scalar_tensor_tensor(
                    out=lt[0][:, 0:ntc], in0=lt[0][:, 0:ntc], scalar=-1.0, in1=lt[2][:, 0:ntc],
                    op0=ALU.mult, op1=ALU.mult), reads=[lt_b[0], lt_b[2]], writes=[lt_b[0]])
                dump("vn", td, vn[:, 0:nch, 0:1024], vn_b)

            def evac_u(oc, bk, bk_b):
                act.op(lambda: nc.scalar.activation(out=ubz[:, oc, 0:ntc], in_=bk[:, 0:ntc], func=AF.Gelu_apprx_tanh),
                       reads=[bk_b], writes=[ubz_b])
            stats_b()
            proj_like(12, lambda kc: h[:, kc, M0:M1], [h_b], evac_u)
            def evac_bz(oc, bk, bk_b):
                k = oc % 2
                act.op(lambda: nc.scalar.activation(out=stmp[k][:, 0:ntc], in_=bk[:, 0:ntc], func=AF.Silu),
                       reads=[bk_b], writes=[stmp_b[k]])
                dve.op(lambda: nc.vector.tensor_tensor(
                    out=ubz[:, oc, 0:ntc], in0=ubz[:, oc, 0:ntc], in1=stmp[k][:, 0:ntc], op=ALU.mult),
                    reads=[ubz_b, stmp_b[k]], writes=[ubz_b])
            hb = hg

            def evac_bz_sgu(g, bk, bk_b):
                evac_bz(g, bk, bk_b)
                b2, b2_b = PB.next()
                wd = [min(128, ntc - tc * 128) for tc in range(nch)]
                fns = [lambda tc=tc: nc.tensor.matmul(
                    b2[:, tc * 128:tc * 128 + wd[tc]], vn[:, tc, g * 128:(g + 1) * 128], wst[:, g, 0:wd[tc]],
                    start=True, stop=True) for tc in range(nch)]
                pe.group(fns, reads=[vn_b, wst_b], writes=[b2_b])
                k = g % 2
                nfull = nch - 1 if thin else nch
                dve.op(lambda: nc.vector.scalar_tensor_tensor(
                    out=mt[k][:, 0:nfull * 128].rearrange("p (a b) -> p a b", b=128),
                    in0=b2[:, 0:nfull * 128].rearrange("p (a b) -> p a b", b=128), scalar=pvec[:, l0, 4, g:g + 1],
                    in1=BB[:, g, :].unsqueeze(1).broadcast_to([128, nfull, 128]), op0=ALU.mult, op1=ALU.add),
                    reads=[b2_b, BB_b, par_b], writes=[mt_b[k]])
                if thin:
                    dve.op(lambda: nc.vector.scalar_tensor_tensor(
                        out=mt[k][:, nfull * 128:ntc], in0=b2[:, nfull * 128:ntc], scalar=pvec[:, l0, 4, g:g + 1],
                        in1=BB[:, g, 0:ntc - nfull * 128], op0=ALU.mult, op1=ALU.add),
                        reads=[b2_b, BB_b, par_b, mt_b[k]], writes=[mt_b[k]])
                dve.op(lambda: nc.vector.tensor_tensor(
                    out=hb[:, g, 0:ntc], in0=mt[k][:, 0:ntc], in1=ubz[:, g, 0:ntc], op=ALU.mult),
                    reads=[mt_b[k], ubz_b], writes=[hgp_b[g // 2]])
            proj_like(20, lambda kc: h[:, kc, M0:M1], [h_b], evac_bz_sgu)
            dve.op(lambda: nc.vector.tensor_tensor(
                out=xc, in0=xc, in1=lt[2][:, 0:ntc].unsqueeze(1).broadcast_to([128, KC, ntc]), op=ALU.mult),
                reads=[bigA_b, lt_b[2]], writes=[bigA_b])
            dve.op(lambda: nc.vector.tensor_tensor(
                out=xc, in0=xc, in1=lt[0][:, 0:ntc].unsqueeze(1).broadcast_to([128, KC, ntc]), op=ALU.add),
                reads=[bigA_b, lt_b[0]], writes=[bigA_b])
            if nxt is not None:
                pre_scale(nxt)
            proj_like(8, lambda kc: h[:, kc, M0:M1], [h_b],
                      lambda oc, bk, bk_b: act.op(
                          lambda: nc.scalar.activation(out=saz[:, oc, 0:ntc], in_=bk[:, 0:ntc], func=AF.Silu),
                          reads=[bk_b], writes=[saz_b]))
            dump("hb", td, hb[:, :, 0:ntc], hgp_b[3])
            dump("saz", td, saz[:, :, 0:ntc], saz_b)
            dump("sga", td, sga[:, :, 0:ntc], sga_b)
            ha = ubz
            for c in range(KC):
                act.op(lambda c=c: nc.scalar.activation(
                    out=ha[:, c, 0:ntc], in_=xc[:, c, :], func=AF.Silu, scale=pvec[:, l0, 2, c:c + 1],
                    bias=pvec[:, l0, 3, c:c + 1]), reads=[bigA_b, par_b], writes=[ubz_b])
            if nxt is not None:
                pre_affine(nxt)
            tb = v8(bigB, ntc)
            def evac_tb(oc, bk, bk_b):
                dve.op(lambda: nc.vector.tensor_tensor(
                    out=tb[:, oc, :], in0=bk[:, 0:ntc], in1=sgb[:, oc, 0:ntc], op=ALU.mult),
                    reads=[bk_b, sgb_b], writes=[bigB_b])
                dve.op(lambda: nc.vector.tensor_tensor(
                    out=ha[:, oc, 0:ntc], in0=ha[:, oc, 0:ntc], in1=saz[:, oc, 0:ntc], op=ALU.mult),
                    reads=[ubz_b, saz_b], writes=[ubz_b])
            proj_like(36, lambda kc: hb[:, kc, 0:ntc], hgp_b, evac_tb)
            dump("ha", td, ha[:, :, 0:ntc], ubz_b)
            mv = v8(vnf, ntc)
            mch_b = [Buf() for _ in range(KC)]

            def evac_m(oc, bk, bk_b):
                k = oc % 2
                dve.op(lambda: nc.vector.tensor_tensor(
                    out=tt[k][:, 0:ntc], in0=bk[:, 0:ntc], in1=sga[:, oc, 0:ntc], op=ALU.mult),
                    reads=[bk_b, sga_b], writes=[tt_b[k]])
                dve.op(lambda: nc.vector.tensor_tensor(
                    out=mv[:, oc, :], in0=tt[k][:, 0:ntc], in1=tb[:, oc, :], op=ALU.add),
                    reads=[tt_b[k], bigB_b], writes=[vn_b, mch_b[oc]])
            proj_like(32, lambda kc: ha[:, kc, 0:ntc], [ubz_b], evac_m)
            dump("m", td, mv, vn_b)
            xr = v8(bigB, ntc)
            rd = xrange_bufs(s, l - 1, t0, t0 + ntc)
            qs.dma(xr, Xv[s][l - 1][:, :, t0:t0 + ntc], reads=rd, writes=[bigB_b])
            of = v8(bigA, ntc)

            def evac_o(oc, bk, bk_b):
                act.op(lambda: nc.scalar.activation(out=sq2v[:, oc, 0:ntc], in_=bk[:, 0:ntc], func=AF.Square),
                       reads=[bk_b], writes=[sq2_b])
                act.op(lambda: nc.scalar.activation(out=of[:, oc, :], in_=bk[:, 0:ntc], func=AF.Identity,
                                                    scale=Gall[:, l0, s, oc:oc + 1]),
                       reads=[bk_b, par_b, mod_b[l0]], writes=[bigA_b])
            for q in range(4):
                wt, wt_b = need_group("w", l0, 40 + q)
                for jj in range(2):
                    oc = q * 2 + jj
                    bk, bk_b = PA.next()
                    fns = mm_chunk(wt, jj * 128, lambda kc: mv[:, kc, :], bk[:, 0:ntc])
                    if oc == 0:
                        pe.group(fns[:KC - 1], reads=[wt_b] + mch_b[:KC - 1], writes=[bk_b])
                        _merge(vn_b.r, {pe.sem: pe.cnt})
                        pe.group(fns[KC - 1:], reads=[wt_b, vn_b], writes=[bk_b])
                    else:
                        pe.group(fns, reads=[wt_b, vn_b], writes=[bk_b])
                    evac_o(oc, bk, bk_b)
            dump("of", td, of, bigA_b)
            post_tok = []

            def post_a():
                post_tok.append(chan_sum_a(lambda kc: sq2v[:, kc, 0:ntc], ntc, [sq2_b]))

            def post_b():
                chan_sum_b(post_tok[0], S2[:, 0:ntc], S2_b)
                act.op(lambda: nc.scalar.activation(out=lt[2][:, 0:ntc], in_=S2[:, 0:ntc], func=AF.Ln,
                                                    bias=eps_rms[:, 0:1], scale=1.0 / D),
                       reads=[S2_b, par_b], writes=[lt_b[2]])
                act.op(lambda: nc.scalar.activation(out=lt[2][:, 0:ntc], in_=lt[2][:, 0:ntc], func=AF.Exp,
                                                    scale=-0.5), reads=[lt_b[2]], writes=[lt_b[2]])

            def tail():
                dve.op(lambda: nc.vector.tensor_tensor(
                    out=of, in0=of, in1=lt[2][:, 0:ntc].unsqueeze(1).broadcast_to([128, KC, ntc]), op=ALU.mult),
                    reads=[bigA_b, lt_b[2]], writes=[bigA_b])
                dve.op(lambda: nc.vector.tensor_tensor(out=of, in0=of, in1=xr, op=ALU.add),
                       reads=[bigA_b, bigB_b], writes=[bigA_b])
                nst = min(ntc, Xlen[s][l] - t0)
                if nst > 0:
                    qg.dma(Xv[s][l][:, :, t0:t0 + nst], of[:, :, 0:nst], reads=[bigA_b],
                           writes=xrange_bufs(s, l, t0, t0 + nst))
            return (post_a, post_b, tail)

        cur_layer = 0
        first = tiles[0]
        pre_load(first)
        pre_square(first)
        pre_stats_a(first)
        pre_stats_b(first)
        pre_sqrt(first)
        pre_scale(first)
        emit_mod(0)
        pre_affine(first)
        ntl = [sum(1 for t in tiles if t["l"] == l) for l in range(1, DEPTH + 1)]
        prev_tail = None
        for ti, td in enumerate(tiles):
            if td["l"] != cur_layer:
                cur_layer = td["l"]
                layer_setup(cur_layer - 1)
            nxt = tiles[ti + 1] if ti + 1 < len(tiles) else None
            if nxt is not None and nxt["l"] != td["l"]:
                emit_mod(nxt["l"] - 1)
            if nxt is not None:
                pre_load(nxt)
            prev_tail = tile_body(td, nxt, prev_tail)
            per = (NGRP + 12 + ntl[cur_layer - 1] - 1) // ntl[cur_layer - 1] + 2
            issue_conversions(cur_layer, per)
        for f in prev_tail:
            f()
        fin = {}
        _merge(fin, qg.final_tokens())
        _merge(fin, qs.final_tokens())
        _merge(fin, qa.final_tokens())
        sp._wait(fin)
        pool._wait(dict(fin))
    return nc


def _prep_shared(inp, depth):
    f = np.float32
    sh = {}
    for k in ("w_ada", "w_in", "conv_proj", "sgu_proj", "w_out"):
        sh[k] = np.ascontiguousarray(inp[k][:depth], dtype=f)
    sh["bada"] = np.ascontiguousarray(inp["b_ada"][:depth].reshape(depth, 24, 128).transpose(2, 0, 1), dtype=f)
    pv = np.stack([inp[k][:depth] for k in ("g_pre", "conv_b", "conv_ln_g", "conv_ln_b", "sgu_ln_g", "sgu_ln_b",
                                            "g_post")], axis=1)
    sh["pvec"] = np.ascontiguousarray(pv.reshape(depth, 7, 8, 128).transpose(3, 0, 1, 2), dtype=f)
    m = np.zeros((128, 32), f)
    for s in range(4):
        m[32 * s + np.arange(32), np.arange(32)] = 1.0
    sh["mask"] = m
    return sh


def _prep_flip(inp, depth, flip):
    f = np.float32
    out = {}
    cw = inp["conv_w"][:depth]
    if flip:
        cw = cw[:, ::-1, :]
    cwp = np.zeros((depth, 32, 1024), f)
    cwp[:, :31] = cw
    a = cwp.reshape(depth, 8, 4, 8, 4, 32)
    out["wcol"] = np.ascontiguousarray(a.transpose(2, 5, 0, 3, 4, 1).reshape(128, depth, 256), dtype=f)
    ws = inp["sgu_ws"][:depth]
    bs = inp["sgu_bs"][:depth]
    if flip:
        ws = ws[:, :, ::-1, ::-1]
        bs = bs[:, :, ::-1]
    out["wst"] = np.ascontiguousarray(ws.transpose(3, 0, 1, 2), dtype=f)
    out["bsb"] = np.ascontiguousarray(np.broadcast_to(bs[None], (128,) + bs.shape), dtype=f)
    return out


def run(inp, cfg, seqs):
    depth = cfg.depth
    TS, TP = cfg.T
    nc = build_nc(cfg)
    shared = _prep_shared(inp, depth)
    flips = [_prep_flip(inp, depth, False), _prep_flip(inp, depth, True)]
    in_maps = []
    for (si, pi) in seqs:
        for half in range(2):
            mp = dict(shared)
            mp.update(flips[half])
            xs = []
            for (x, T) in ((inp["x_sample"][si], TS), (inp["x_prompt"][pi], TP)):
                if half == 1:
                    x = x[::-1]
                L1 = T + depth * 128
                xs.append(np.ascontiguousarray(x[:L1].T, dtype=np.float32))
            mp["x0"], mp["x1"] = xs
            c2 = np.stack([inp["c_sample"][si], inp["c_prompt"][pi]], axis=-1)
            mp["cvec"] = np.ascontiguousarray(c2.reshape(8, 128, 2).transpose(1, 0, 2), dtype=np.float32)
            in_maps.append(mp)
    res = run_bass_kernel_spmd(nc, in_maps, core_ids=list(range(len(in_maps))))
    if cfg.debug:
        run.dbg = [r["dbg"] for r in res.results]
    ys = np.zeros((len(seqs), 2 * TS, D), np.float32)
    yp = np.zeros((len(seqs), 2 * TP, D), np.float32)
    for n, (si, pi) in enumerate(seqs):
        for half in range(2):
            r = res.results[2 * n + half]
            a = r["y0"].T
            b = r["y1"].T
            if half == 0:
                ys[n, :TS] = a
                yp[n, :TP] = b
            else:
                ys[n, TS:] = a[::-1]
                yp[n, TP:] = b[::-1]
    return yp, ys


def kernel(**inputs):
    inp = {k: np.asarray(v) for k, v in inputs.items()}
    cfg = Cfg(depth=4, T=(4096, 2048))
    yp, ys = run(inp, cfg, [(i, i) for i in range(4)])
    return (yp.astype(np.float32), ys.astype(np.float32))
```
